# Optimizing a Trainium2 kernel written in Bass

```python
import jax, jax.numpy as jnp
from jax import lax
import numpy as np

D_MODEL = 1024
BATCH = 8
SEQ = 4096
DEPTH = 1

CHUNK = 64
Q_BLOCK = 128
EPS = 1e-6

N_HEADS_A = 8
N_KV_A = 2
HEAD_DIM = 64
ROT_DIM = HEAD_DIM // 4
ROPE_THETA = 500000.0
N_IDX_HEADS = 8
IDX_DIM = 32
IDX_ROT_DIM = IDX_DIM // 4
TOPK_MAX = 256
WIDTH_A = N_HEADS_A * HEAD_DIM

POOL_WINDOWS = (2, 4, 8, 16)
N_POOL_GROUPS = 4
POOL_GROUP_DIM = 128
WIDTH_B = N_POOL_GROUPS * POOL_GROUP_DIM

N_BRANCHES = 2

SPLIT_SIZES = (
    WIDTH_A,
    N_KV_A * HEAD_DIM,
    N_KV_A * HEAD_DIM,
    N_IDX_HEADS * IDX_DIM,
    IDX_DIM,
    N_IDX_HEADS,
    WIDTH_A,
    WIDTH_B,
    WIDTH_B,
    N_BRANCHES * D_MODEL,
)
D_IN = int(sum(SPLIT_SIZES))
SPLIT_POINTS = [int(s) for s in np.cumsum(SPLIT_SIZES)[:-1]]

kernel_name = "hybrid_dsa_pool_gated_block"


def rms_norm(t, g):
    tf = t.astype(jnp.float32)
    tf = tf * lax.rsqrt(jnp.mean(tf * tf, axis=-1, keepdims=True) + EPS)
    return (tf * g.astype(jnp.float32)).astype(t.dtype)


def rope_tables(positions, rot_dim):
    half = rot_dim // 2
    inv_freq = ROPE_THETA ** (-jnp.arange(half, dtype=jnp.float32) / half)
    ang = positions.astype(jnp.float32)[..., None] * inv_freq
    return jnp.cos(ang), jnp.sin(ang)


def apply_partial_rope(t, cos, sin):
    half = cos.shape[-1]
    tf = t.astype(jnp.float32)
    c = cos[:, :, None, :]
    s = sin[:, :, None, :]
    t1 = tf[..., :half]
    t2 = tf[..., half:2 * half]
    out = jnp.concatenate([t1 * c - t2 * s, t1 * s + t2 * c, tf[..., 2 * half:]], axis=-1)
    return out.astype(t.dtype)


def dsa_attention(q, k, v, qi, ki, wi, k_sel):
    B, S = q.shape[0], q.shape[1]
    n_blocks = S // Q_BLOCK
    rep = N_HEADS_A // N_KV_A
    key_chunk = jnp.arange(S) // CHUNK
    scale = HEAD_DIM ** -0.5

    def block(i):
        start = i * Q_BLOCK
        qb = lax.dynamic_slice_in_dim(q, start, Q_BLOCK, axis=1)
        qib = lax.dynamic_slice_in_dim(qi, start, Q_BLOCK, axis=1)
        wib = lax.dynamic_slice_in_dim(wi, start, Q_BLOCK, axis=1)
        q_chunk = (start + jnp.arange(Q_BLOCK)) // CHUNK
        allowed = key_chunk[None, :] <= q_chunk[:, None]
        logits = jnp.einsum('bqhd,bsd->bqhs', qib, ki)
        iscore = jnp.einsum('bqhs,bqh->bqs', jax.nn.relu(logits), wib).astype(jnp.float32)
        iscore = jnp.where(allowed[None], iscore, -jnp.inf)
        _, idx = lax.top_k(iscore, k_sel)
        valid = key_chunk[idx] <= q_chunk[None, :, None]
        kg = jax.vmap(lambda kb, ib: kb[ib])(k, idx)
        vg = jax.vmap(lambda vb, ib: vb[ib])(v, idx)
        qg = qb.reshape(B, Q_BLOCK, N_KV_A, rep, HEAD_DIM)
        s = jnp.einsum('bqgrd,bqkgd->bqgrk', qg, kg).astype(jnp.float32) * scale
        s = jnp.where(valid[:, :, None, None, :], s, -jnp.inf)
        p = jax.nn.softmax(s, axis=-1).astype(v.dtype)
        o = jnp.einsum('bqgrk,bqkgd->bqgrd', p, vg)
        return o.reshape(B, Q_BLOCK, N_HEADS_A * HEAD_DIM)

    out = lax.map(block, jnp.arange(n_blocks))
    return out.transpose(1, 0, 2, 3).reshape(B, S, N_HEADS_A * HEAD_DIM)


def multiscale_pool(u, pool_w, pool_scale):
    B, S = u.shape[0], u.shape[1]
    ug = u.astype(jnp.float32).reshape(B, S, N_POOL_GROUPS, POOL_GROUP_DIM)
    cs = jnp.cumsum(ug, axis=1)
    t = jnp.arange(S)
    outs = []
    for g, w in enumerate(POOL_WINDOWS):
        c = cs[:, :, g]
        lagged = jnp.pad(c, ((0, 0), (w, 0), (0, 0)))[:, :S]
        count = jnp.minimum(t + 1, w).astype(jnp.float32)[None, :, None]
        outs.append((c - lagged) / count - ug[:, :, g])
    pooled = jnp.stack(outs, axis=2).astype(u.dtype)
    mixed = jnp.einsum('bsgc,gcd->bsgd', pooled, pool_w)
    return mixed.reshape(B, S, WIDTH_B) * pool_scale


def setup_inputs(seed: int = 0) -> dict:
    key = jax.random.key(seed)
    ks = jax.random.split(key, 13)
    f32 = jnp.float32
    x = jax.random.normal(ks[0], (BATCH, SEQ, D_MODEL), f32)
    offsets = jax.random.randint(ks[1], (BATCH, 1), 0, 100000, dtype=jnp.int32)
    positions = (jnp.arange(SEQ, dtype=jnp.int32)[None, :] + offsets).astype(jnp.int32)
    norm_g = 1.0 + 0.02 * jax.random.normal(ks[2], (DEPTH, D_MODEL), f32)
    w_in = jax.random.normal(ks[3], (DEPTH, D_MODEL, D_IN), f32) * D_MODEL ** -0.5
    merge_bias = 0.02 * jax.random.normal(ks[4], (DEPTH, N_BRANCHES, D_MODEL), f32)
    q_norm_g = 1.0 + 0.02 * jax.random.normal(ks[5], (DEPTH, HEAD_DIM), f32)
    k_norm_g = 1.0 + 0.02 * jax.random.normal(ks[6], (DEPTH, HEAD_DIM), f32)
    pool_w = jax.random.normal(ks[7], (DEPTH, N_POOL_GROUPS, POOL_GROUP_DIM, POOL_GROUP_DIM), f32) * POOL_GROUP_DIM ** -0.5
    pool_scale = 1.0 + 0.1 * jax.random.normal(ks[8], (DEPTH, WIDTH_B), f32)
    w_branch_a = jax.random.normal(ks[9], (DEPTH, WIDTH_A, D_MODEL), f32) * WIDTH_A ** -0.5
    w_branch_b = jax.random.normal(ks[10], (DEPTH, WIDTH_B, D_MODEL), f32) * WIDTH_B ** -0.5
    w_out = jax.random.normal(ks[11], (DEPTH, D_MODEL, D_MODEL), f32) * D_MODEL ** -0.5
    return {"x": x, "positions": positions, "norm_g": norm_g, "w_in": w_in,
            "merge_bias": merge_bias, "q_norm_g": q_norm_g, "k_norm_g": k_norm_g,
            "pool_w": pool_w, "pool_scale": pool_scale, "w_branch_a": w_branch_a,
            "w_branch_b": w_branch_b, "w_out": w_out}


def reference(x, positions, norm_g, w_in, merge_bias, q_norm_g, k_norm_g, pool_w, pool_scale,
              w_branch_a, w_branch_b, w_out):
    B, S = x.shape[0], x.shape[1]
    k_sel = min(TOPK_MAX, S // 4)
    cos_a, sin_a = rope_tables(positions, ROT_DIM)
    cos_i, sin_i = rope_tables(positions, IDX_ROT_DIM)
    for layer in range(DEPTH):
        h = rms_norm(x, norm_g[layer])
        proj = h @ w_in[layer]
        q, k, v, qi, ki, wi, z_a, u_b, z_b, gates = jnp.split(proj, SPLIT_POINTS, axis=-1)

        q = q.reshape(B, S, N_HEADS_A, HEAD_DIM)
        k = k.reshape(B, S, N_KV_A, HEAD_DIM)
        v = v.reshape(B, S, N_KV_A, HEAD_DIM)
        q = apply_partial_rope(rms_norm(q, q_norm_g[layer]), cos_a, sin_a)
        k = apply_partial_rope(rms_norm(k, k_norm_g[layer]), cos_a, sin_a)
        qi = apply_partial_rope(qi.reshape(B, S, N_IDX_HEADS, IDX_DIM), cos_i, sin_i) * (IDX_DIM ** -0.5)
        ki = apply_partial_rope(ki[:, :, None, :], cos_i, sin_i)[:, :, 0, :]
        wi = wi * (N_IDX_HEADS ** -0.5)
        a = dsa_attention(q, k, v, qi, ki, wi, k_sel) * jax.nn.silu(z_a)

        b = multiscale_pool(u_b, pool_w[layer], pool_scale[layer]) * jax.nn.silu(z_b)

        g = jax.nn.sigmoid(gates.reshape(B, S, N_BRANCHES, D_MODEL) + merge_bias[layer])
        y = g[:, :, 0] * (a @ w_branch_a[layer]) + g[:, :, 1] * (b @ w_branch_b[layer])
        x = x + y @ w_out[layer]
    return x
```

```python
import numpy as np
from contextlib import ExitStack
import concourse.bass as bass
import concourse.mybir as mybir
from concourse.bass_utils import run_bass_kernel_spmd

F32 = mybir.dt.float32
BF16 = mybir.dt.bfloat16
I32 = mybir.dt.int32
ALU = mybir.AluOpType
AF = mybir.ActivationFunctionType
AX = mybir.AxisListType

S = 4096
D = 1024
NT = 32
NIT = 14
TOPK = 256
EPS = 1e-6
NEG_MASK = -30000.0
TWO_PI = 2.0 * np.pi
CW1 = 6.28125
CW2 = float(np.float32(TWO_PI - 6.28125))
MAGIC = 12582912.0

CF = {}
_off = 0
for _n, _w in [("gnorm", 1024), ("gq", 64), ("gk", 64), ("invfa", 8), ("invfi", 4),
               ("H2", 2), ("cmask", 128), ("invcnt", 64), ("bis_s", NIT), ("bis_t", NIT),
               ("mbias", 16), ("pscale", 4), ("E64", 128), ("halfpi", 1), ("zero", 1), ("eps", 1), ("onesf", 128), ("cmaskb", 128)]:
    CF[_n] = (_off, _w)
    _off += _w
NCF = _off
CB = {}
_off = 0
for _n, _w in [("ident", 128), ("I4", 512), ("Dsel2", 256), ("ones", 128)]:
    CB[_n] = (_off, _w)
    _off += _w
NCB = _off


class Res:
    __slots__ = ("name", "w", "r")

    def __init__(self, name):
        self.name = name
        self.w = None
        self.r = []


class Prog:
    COMPUTE = ("pe", "act", "dve", "pool")

    def __init__(self, nc, es):
        self.nc = nc
        self.es = es
        self.q = {e: [] for e in ("pe", "act", "dve", "pool", "sp")}
        self.nops = {}
        self.waited = {e: {} for e in self.q}
        self.needed = {}
        self.sems = {}

    def _deps(self, eng, reads, writes):
        need = {}
        for r in reads:
            if r.w is not None:
                k, v = r.w
                need[k] = max(need.get(k, 0), v)
        for w in writes:
            if w.w is not None:
                k, v = w.w
                if not (k == eng and eng == "pe"):
                    need[k] = max(need.get(k, 0), v)
            for (k, v) in w.r:
                need[k] = max(need.get(k, 0), v)
        out = []
        for k, v in need.items():
            if k == eng and eng == "pe":
                continue
            if self.waited[eng].get(k, 0) >= v:
                continue
            self.waited[eng][k] = v
            out.append((k, v))
            self.needed.setdefault(k, set()).add(v)
        return out

    def capture_begin(self):
        self.cap = []

    def capture_end(self):
        c, self.cap = self.cap, None
        return c

    def play(self, items):
        for it in items:
            self.op(*it)

    def op(self, eng, fn, reads=(), writes=(), chan=None):
        if getattr(self, "cap", None) is not None:
            self.cap.append((eng, fn, list(reads), list(writes), chan))
            return None
        waits = self._deps(eng, reads, writes)
        key = chan if chan is not None else eng
        self.nops[key] = self.nops.get(key, 0) + 1
        tok = (key, self.nops[key])
        self.q[eng].append((waits, fn, tok))
        for r in reads:
            r.r.append(tok)
        for w in writes:
            w.w = tok
            w.r = []
        return tok

    def barrier(self):
        latest = [(k, n) for k, n in self.nops.items()]
        for eng in self.q:
            waits = []
            for (k, v) in latest:
                if k == eng:
                    continue
                if self.waited[eng].get(k, 0) >= v:
                    continue
                self.waited[eng][k] = v
                waits.append((k, v))
                self.needed.setdefault(k, set()).add(v)
            if waits:
                self.q[eng].append((waits, None, None))

    def guard(self, eng, writes):
        return None

    def final_wait(self, eng, toks):
        waits = []
        for (k, v) in toks:
            waits.append((k, v))
            self.needed.setdefault(k, set()).add(v)
        self.q[eng].append((waits, None, None))

    def emit(self):
        nc = self.nc
        keys = set(self.nops.keys())
        for k in sorted(keys):
            self.sems[k] = self.es.enter_context(nc.semaphore("s_" + k))
        rank = {}
        for k in keys:
            if k in self.COMPUTE:
                vals = sorted(self.needed.get(k, ()))
                rank[k] = {v: i + 1 for i, v in enumerate(vals)}
        sems = self.sems

        def val(k, v):
            if k in self.COMPUTE:
                return rank[k][v]
            return 16 * v

        def replay(name):
            def body(e):
                for waits, fn, tok in self.q[name]:
                    for (k, v) in waits:
                        e.wait_ge(sems[k], val(k, v))
                    if fn is None:
                        continue
                    inst = fn(e)
                    k, v = tok
                    if k in self.COMPUTE:
                        if v in rank[k]:
                            inst.then_inc(sems[k], 1)
                    else:
                        inst.then_inc(sems[k], 16)
            return body

        with nc.Block() as block:
            block.sync(replay("sp"))
            block.tensor(replay("pe"))
            block.scalar(replay("act"))
            block.vector(replay("dve"))
            block.gpsimd(replay("pool"))


def _perm_q():
    idx = []
    for r in range(4):
        for g in range(2):
            for d in range(64):
                idx.append((g * 4 + r) * 64 + d)
    return np.array(idx, dtype=np.int64)


def host_consts():
    cf = np.zeros((128, NCF), np.float32)
    cb = np.zeros((128, NCB), np.float32)
    p = np.arange(128)

    def put(name, arr):
        o, w = CF[name]
        cf[:, o:o + w] = arr

    half = 8
    invfa = (np.float32(500000.0) ** (-(np.arange(half, dtype=np.float32) / np.float32(half)))).astype(np.float32)
    half = 4
    invfi = (np.float32(500000.0) ** (-(np.arange(half, dtype=np.float32) / np.float32(half)))).astype(np.float32)
    put("invfa", invfa[None, :])
    put("invfi", invfi[None, :])
    put("H2", (p[:, None] // 64 == np.arange(2)[None, :]).astype(np.float32))
    cm = np.zeros((128, 128), np.float32)
    cm[:64, 64:] = -1e30
    put("cmask", cm)
    ic = np.zeros((4, 16), np.float32)
    for g, w in enumerate((2, 4, 8, 16)):
        ic[g] = 1.0 / np.minimum(np.arange(16) + 1, w)
    put("invcnt", ic.reshape(1, 64))
    n = np.arange(1, NIT + 1, dtype=np.float64)
    step = 2.0 * 2.0 ** (-n)
    bs = step.copy()
    bt = np.empty(NIT)
    bt[:-1] = -step[1:]
    bt[-1] = -step[-1]
    put("bis_s", bs[None, :].astype(np.float32))
    put("bis_t", bt[None, :].astype(np.float32))
    put("E64", (p[:, None] % 64 == (np.arange(128)[None, :] % 64)).astype(np.float32))
    put("halfpi", np.float32(np.pi / 2))
    put("zero", 0.0)
    put("eps", np.float32(EPS))
    sel = np.zeros((128, 128), np.float32)
    sel[0, :] = 1.0
    sel[64, :] = 1.0
    put("onesf", sel)
    put("cmaskb", np.where(cm < -1e29, NEG_MASK, 0.0).astype(np.float32))

    def putb(name, arr):
        o, w = CB[name]
        cb[:, o:o + w] = arr

    putb("ident", np.eye(128, dtype=np.float32))
    putb("I4", np.tile(np.eye(128, dtype=np.float32), (1, 4)))
    d2 = np.zeros((128, 2, 128), np.float32)
    for m in range(128):
        tl = m % 64
        for th in range(2):
            d2[m, th, th * 64 + tl] = 1.0
    putb("Dsel2", d2.reshape(128, 256))
    putb("ones", 1.0)
    return cf, cb


def build(dbg=()):
    dbg = set(dbg)
    nc = bass.Bass("TRN2", target_bir_lowering=False)

    def din(name, shape, dt=F32):
        return nc.dram_tensor(name, list(shape), dt, kind="ExternalInput").ap()

    x = din("x", [S, D])
    posT = din("posT", [128, NT], I32)
    cf_d = din("cf", [128, NCF])
    cb_d = din("cb", [128, NCB])
    w1_d = din("w1", [D, 1064])
    w3_d = din("w3", [D, 3584])
    wa_d = din("wa", [512, D])
    wb_d = din("wb", [512, D])
    wo_d = din("wo", [D, D])
    pw_d = din("pw", [512, 128])
    out_d = nc.dram_tensor("out", [S, D], F32, kind="ExternalOutput").ap()
    dbg_d = {}

    def dout(name, shape, dt=F32):
        dbg_d[name] = nc.dram_tensor(name, list(shape), dt, kind="ExternalOutput").ap()
        return dbg_d[name]

    es = ExitStack()
    with es:
        P = Prog(nc, es)

        scope = {"es": es}

        def sb(name, shape, dt):
            return scope["es"].enter_context(nc.sbuf_tensor("sb_" + name, list(shape), dt))

        banks = [es.enter_context(nc.psum_tensor(f"bank{i}", [128, 512], F32)) for i in range(8)]
        bres = [Res(f"bank{i}") for i in range(8)]

        cf = sb("cf", [128, NCF], F32)
        cb = sb("cb", [128, NCB], BF16)
        r_cf, r_cb = Res("cf"), Res("cb")
        P.op("sp", lambda e: e.dma_start(out=cf[:], in_=cf_d), writes=[r_cf], chan="d_cf")
        P.op("pool", lambda e: e.dma_start(out=cb[:], in_=cb_d), writes=[r_cb], chan="d_cb")

        def cfs(name, a=0, b=None):
            o, w = CF[name]
            b = w if b is None else b
            return cf[:, o + a:o + b]

        def cbs(name, a=0, b=None):
            o, w = CB[name]
            b = w if b is None else b
            return cb[:, o + a:o + b]

        ident = cbs("ident")
        gscr_d = sb("gscr_d", [128, 8], F32)
        gscr_a = sb("gscr_a", [128, 8], F32)
        gscr_a2 = sb("gscr_a2", [128, 8], F32)
        P.op("act", lambda e: e.activation(out=gscr_a2[:], in_=cfs("zero").to_broadcast([128, 8]), func=AF.Copy), reads=[r_cf])
        P.guard_fn = {"dve": lambda e: e.memset(gscr_d[:], 0.0),
                      "act": lambda e: e.activation(out=gscr_a[:], in_=gscr_a2[:], func=AF.Copy)}

        A1 = 4 * 4096 + 2 * 4096 + 2 * 4096 + 4096 + NT * 193 + 8 * 1064
        arena = sb("arena", [128, max(A1, 8 * 3584 + 2 * 4 * 1024 + 8 * 1024 + 4 * 128)], BF16)
        o = 0
        qT = arena[:, o:o + 4 * S].rearrange("p (r t) -> p r t", r=4); o += 4 * S
        kTz = arena[:, o:o + 2 * S].rearrange("p (g t) -> p g t", g=2); o += 2 * S
        qiT = arena[:, o:o + 2 * S].rearrange("p (b h t) -> p b h t", h=2, t=64); o += 2 * S
        kiT4 = arena[:, o:o + S]; o += S
        v_sb = arena[:, o:o + NT * 193].rearrange("p (n c) -> p n c", n=NT); o += NT * 193
        W1_flat = arena[:, o:o + 8 * 1064]
        W1 = W1_flat.rearrange("p (k c) -> p k c", k=8); o += 8 * 1064
        r_qT, r_kT, r_qiT, r_kiT, r_v, r_W1 = (Res(n) for n in ("qT", "kT", "qiT", "kiT", "v", "W1"))
        ph12 = [r_qT, r_kT, r_qiT, r_kiT, r_v]

        aT = sb("aT", [128, 4, S], BF16)
        r_aT = Res("aT")
        es_p12 = ExitStack()
        scope["es"] = es_p12
        witm_all = sb("witm_all", [128, NT, 8], F32)
        wexp_all = sb("wexp_all", [128, NT, 16], F32)
        wabs = sb("wabs", [128, NT, 8], F32)
        wsgn = sb("wsgn", [128, NT, 8], F32)
        r_wabs = [Res(f"wabs{i}") for i in range(NT)]
        cosA = sb("cosA", [128, NT, 8], F32)
        sinA = sb("sinA", [128, NT, 8], F32)
        cosI = sb("cosI", [128, NT, 4], F32)
        sinI = sb("sinI", [128, NT, 4], F32)
        r_rope = Res("rope")

        w1v = w1_d.rearrange("(k p) c -> p k c", p=128)
        r_W1c = [Res(f"W1c{j}") for j in range(4)]
        for kc in range(0, 8, 2):
            P.op("pool", lambda e, kc=kc: e.dma_start(out=W1[:, kc:kc + 2, :], in_=w1v[:, kc:kc + 2, :]),
                 writes=[r_W1c[kc // 2]], chan=f"d_w1_{kc}")

        es0 = ExitStack()
        scope["es"] = es0
        pos_i = sb("pos_i", [128, NT], I32)
        pos_f = sb("pos_f", [128, NT], F32)
        r_pos = Res("pos")
        P.op("sp", lambda e: e.dma_start(out=pos_i[:], in_=posT), writes=[r_pos], chan="d_pos")
        P.op("dve", lambda e: e.tensor_copy(out=pos_f[:], in_=pos_i[:]), reads=[r_pos], writes=[r_pos])
        ang = sb("ang", [128, NT, 12], F32)
        tq = sb("tq", [128, NT, 12], F32)
        nq = sb("nq", [128, NT, 12], F32)
        rq = sb("rq", [128, NT, 12], F32)
        aq = sb("aq", [128, NT, 12], F32)
        r_ang = Res("ang")
        P.op("dve", lambda e: e.tensor_tensor(out=ang[:, :, 0:8], in0=pos_f[:].unsqueeze(2).to_broadcast([128, NT, 8]),
                                              in1=cfs("invfa").unsqueeze(1).to_broadcast([128, NT, 8]), op=ALU.mult),
             reads=[r_pos, r_cf], writes=[r_ang])
        P.op("dve", lambda e: e.tensor_tensor(out=ang[:, :, 8:12], in0=pos_f[:].unsqueeze(2).to_broadcast([128, NT, 4]),
                                              in1=cfs("invfi").unsqueeze(1).to_broadcast([128, NT, 4]), op=ALU.mult),
             reads=[r_pos, r_cf, r_ang], writes=[r_ang])
        P.op("dve", lambda e: e.tensor_scalar(out=tq[:], in0=ang[:], scalar1=float(1.0 / TWO_PI), scalar2=MAGIC,
                                              op0=ALU.mult, op1=ALU.add), reads=[r_ang], writes=[r_ang])
        P.op("dve", lambda e: e.tensor_scalar(out=nq[:], in0=tq[:], scalar1=-MAGIC, scalar2=None, op0=ALU.add),
             reads=[r_ang], writes=[r_ang])
        P.op("dve", lambda e: e.scalar_tensor_tensor(out=rq[:], in0=nq[:], scalar=-CW1, in1=ang[:],
                                                     op0=ALU.mult, op1=ALU.add), reads=[r_ang], writes=[r_ang])
        P.op("dve", lambda e: e.scalar_tensor_tensor(out=tq[:], in0=nq[:], scalar=-CW2, in1=rq[:],
                                                     op0=ALU.mult, op1=ALU.add), reads=[r_ang], writes=[r_ang])
        P.op("dve", lambda e: e.tensor_scalar(out=rq[:], in0=tq[:], scalar1=3.1415925, scalar2=-3.1415925,
                                              op0=ALU.min, op1=ALU.max), reads=[r_ang], writes=[r_ang])
        P.op("act", lambda e: e.activation(out=aq[:], in_=rq[:], func=AF.Abs), reads=[r_ang], writes=[r_ang])
        P.op("act", lambda e: e.activation(out=sinA[:], in_=rq[:, :, 0:8], func=AF.Sin), reads=[r_ang], writes=[r_rope])
        P.op("act", lambda e: e.activation(out=sinI[:], in_=rq[:, :, 8:12], func=AF.Sin), reads=[r_ang, r_rope], writes=[r_rope])
        P.op("act", lambda e: e.activation(out=cosA[:], in_=aq[:, :, 0:8], func=AF.Sin, scale=-1.0, bias=cfs("halfpi")),
             reads=[r_ang, r_rope, r_cf], writes=[r_rope])
        P.op("act", lambda e: e.activation(out=cosI[:], in_=aq[:, :, 8:12], func=AF.Sin, scale=-1.0, bias=cfs("halfpi")),
             reads=[r_ang, r_rope, r_cf], writes=[r_rope])

        es0.close()
        scope["es"] = es_p12
        P.barrier()
        P.op("pool", lambda e: e.memset(v_sb[:, :, 64:66], 1.0), writes=[r_v])
        P.op("pool", lambda e: e.memset(v_sb[:, :, 66:129], 0.0), reads=[r_v], writes=[r_v])

        es1 = ExitStack()
        scope["es"] = es1
        NXB = 2
        xt = [sb(f"xt{i}", [128, D], F32) for i in range(NXB)]
        r_xt = [Res(f"xt{i}") for i in range(NXB)]
        xsq = sb("xsq", [128, D], BF16)
        r_xsq = Res("xsq")
        ssx = [sb(f"ssx{i}", [128, 1], F32) for i in range(NXB)]
        rsx = [sb(f"rsx{i}", [128, 1], F32) for i in range(NXB)]
        r_ssx = [Res(f"ssx{i}") for i in range(NXB)]
        xn = [sb(f"xn{i}", [128, D], BF16) for i in range(NXB)]
        r_xn = [Res(f"xn{i}") for i in range(NXB)]

        def load_norm_tile(tile, slot, xa, rx):
            P.op("sp", lambda e: e.dma_start(out=xa, in_=x[tile * 128:(tile + 1) * 128, :]),
                 writes=[rx], chan=f"d_{rx.name}")
            P.op("act", lambda e: e.activation(out=xsq[:], in_=xa, func=AF.Square, accum_out=ssx[slot][:]),
                 reads=[rx], writes=[r_xsq, r_ssx[slot]])
            P.op("act", lambda e: e.activation(out=rsx[slot][:], in_=ssx[slot][:], func=AF.Sqrt, scale=1.0 / D,
                                               bias=cfs("eps")), reads=[r_ssx[slot], r_cf], writes=[r_ssx[slot]])
            P.op("dve", lambda e: e.reciprocal(out=ssx[slot][:], in_=rsx[slot][:]), reads=[r_ssx[slot]], writes=[r_ssx[slot]])
            P.op("dve", lambda e: e.scalar_tensor_tensor(out=xn[slot][:], in0=xa, scalar=ssx[slot][:], in1=cfs("gnorm"),
                                                         op0=ALU.mult, op1=ALU.mult),
                 reads=[rx, r_ssx[slot], r_cf], writes=[r_xn[slot]])
            P.guard("dve", [r_xn[slot]])

        hTt = [sb(f"hTt{i}", [128, 8, 128], BF16) for i in range(2)]
        r_hTt = [Res(f"hTt{i}") for i in range(2)]
        sq = sb("sq", [128, 640], F32)
        ssq = sb("ssq", [128, 10], F32)
        rr = sb("rr", [128, 10], F32)
        qn = sb("qn", [128, 10, 64], F32)
        qg = sb("qg", [128, 10, 64], F32)
        qr = sb("qr", [128, 10, 64], BF16)
        qik = sb("qik", [128, 9, 32], F32)
        qikr = sb("qikr", [128, 12, 32], BF16)
        ta = sb("ta", [128, 10, 8], F32)
        tb = sb("tb", [128, 10, 8], F32)
        tc2 = sb("tc2", [128, 10, 8], F32)
        td = sb("td", [128, 10, 8], F32)
        ua = sb("ua", [128, 9, 4], F32)
        ub_ = sb("ub_", [128, 9, 4], F32)
        uc = sb("uc", [128, 9, 4], F32)
        ud = sb("ud", [128, 9, 4], F32)
        witm = sb("witm", [128, 8], F32)
        wexp = sb("wexp", [128, 16], F32)
        r_sq, r_ssq, r_qn, r_qg, r_qr, r_qik, r_qikr, r_t, r_u, r_wi = (
            Res(n) for n in ("sq", "ssq", "qn", "qg", "qr", "qik", "qikr", "t", "u", "wi"))

        B_XT, B_T = 0, 7
        xT_ps = banks[B_XT][:].bitcast(BF16)
        t_ps = banks[B_T][:].bitcast(BF16)
        t1_ps = t_ps[:, 0:640]
        t2_ps = t_ps[:, 640:1024]
        B_T1 = B_T2 = B_T

        def phase1_front(i):
            slot = i % NXB
            hs = i % 2
            B_PA, B_PB, B_PC = 1 + i % 2, 3 + i % 2, 5 + i % 2
            load_norm_tile(i, slot, xt[slot][:], r_xt[slot])
            for kc in range(8):
                P.op("pe", lambda e, kc=kc: e.transpose(out=xT_ps[:, kc * 128:(kc + 1) * 128],
                                                        in_=xn[slot][:, kc * 128:(kc + 1) * 128], identity=ident),
                     reads=[r_xn[slot], r_cb], writes=[bres[B_XT]])
            P.op("dve", lambda e: e.tensor_copy(out=hTt[hs][:].rearrange("p k t -> p (k t)"), in_=xT_ps),
                 reads=[bres[B_XT]], writes=[r_hTt[hs]])
            for kc in range(8):
                st, sp_ = (kc == 0), (kc == 7)
                P.op("pe", lambda e, kc=kc, st=st, sp_=sp_: e.matmul(out=banks[B_PA][:], lhsT=hTt[hs][:, kc, :],
                                                                       rhs=W1[:, kc, 0:512], start=st, stop=sp_),
                     reads=[r_hTt[hs], r_W1c[kc // 2]], writes=[bres[B_PA]])
                P.op("pe", lambda e, kc=kc, st=st, sp_=sp_: e.matmul(out=banks[B_PB][:, 0:424], lhsT=hTt[hs][:, kc, :],
                                                                       rhs=W1[:, kc, 512:936], start=st, stop=sp_),
                     reads=[r_hTt[hs], r_W1c[kc // 2]], writes=[bres[B_PB]])
                P.op("pe", lambda e, kc=kc, st=st, sp_=sp_: e.matmul(out=banks[B_PC][:, 0:128], lhsT=hTt[hs][:, kc, :],
                                                                       rhs=W1[:, kc, 936:1064], start=st, stop=sp_),
                     reads=[r_hTt[hs], r_W1c[kc // 2]], writes=[bres[B_PC]])

        def phase1_back(i):
            B_PA, B_PB, B_PC = 1 + i % 2, 3 + i % 2, 5 + i % 2
            PA, PB, PC = banks[B_PA], banks[B_PB], banks[B_PC]
            P.op("act", lambda e: e.activation(out=v_sb[:, i, 0:64], in_=PC[:, 0:64], func=AF.Copy),
                 reads=[bres[B_PC]], writes=[r_v])
            P.op("act", lambda e: e.activation(out=v_sb[:, i, 129:193], in_=PC[:, 64:128], func=AF.Copy),
                 reads=[bres[B_PC], r_v], writes=[r_v])
            P.op("act", lambda e: e.activation(out=sq[:, 0:512], in_=PA[:], func=AF.Square),
                 reads=[bres[B_PA]], writes=[r_sq])
            P.op("act", lambda e: e.activation(out=sq[:, 512:640], in_=PB[:, 0:128], func=AF.Square),
                 reads=[bres[B_PB], r_sq], writes=[r_sq])
            P.op("dve", lambda e: e.tensor_reduce(out=ssq[:], in_=sq[:].rearrange("p (h d) -> p h d", d=64),
                                                  axis=AX.X, op=ALU.add), reads=[r_sq], writes=[r_ssq])
            P.op("act", lambda e: e.activation(out=rr[:], in_=ssq[:], func=AF.Sqrt, scale=1.0 / 64, bias=cfs("eps")),
                 reads=[r_ssq, r_cf], writes=[r_ssq])
            P.op("dve", lambda e: e.reciprocal(out=ssq[:], in_=rr[:]), reads=[r_ssq], writes=[r_ssq])
            P.op("dve", lambda e: e.tensor_tensor(out=qn[:, 0:8, :], in0=PA[:].rearrange("p (h d) -> p h d", d=64),
                                                  in1=ssq[:, 0:8].unsqueeze(2).to_broadcast([128, 8, 64]), op=ALU.mult),
                 reads=[bres[B_PA], r_ssq], writes=[r_qn])
            P.op("dve", lambda e: e.tensor_tensor(out=qn[:, 8:10, :], in0=PB[:, 0:128].rearrange("p (h d) -> p h d", d=64),
                                                  in1=ssq[:, 8:10].unsqueeze(2).to_broadcast([128, 2, 64]), op=ALU.mult),
                 reads=[bres[B_PB], r_ssq, r_qn], writes=[r_qn])
            P.op("dve", lambda e: e.tensor_tensor(out=qg[:, 0:8, :], in0=qn[:, 0:8, :],
                                                   in1=cfs("gq").unsqueeze(1).to_broadcast([128, 8, 64]), op=ALU.mult),
                 reads=[r_qn, r_cf], writes=[r_qg])
            P.op("dve", lambda e: e.tensor_tensor(out=qg[:, 8:10, :], in0=qn[:, 8:10, :],
                                                   in1=cfs("gk").unsqueeze(1).to_broadcast([128, 2, 64]), op=ALU.mult),
                 reads=[r_qn, r_cf, r_qg], writes=[r_qg])
            cA = cosA[:, i, :].unsqueeze(1).to_broadcast([128, 10, 8])
            sA = sinA[:, i, :].unsqueeze(1).to_broadcast([128, 10, 8])
            t1, t2 = qg[:, :, 0:8], qg[:, :, 8:16]
            P.op("act", lambda e: e.activation(out=qr[:, :, 16:64], in_=qg[:, :, 16:64], func=AF.Copy), reads=[r_qg], writes=[r_qr])
            P.op("dve", lambda e: e.tensor_tensor(out=ta[:], in0=t1, in1=cA, op=ALU.mult), reads=[r_qg, r_rope], writes=[r_t])
            P.op("dve", lambda e: e.tensor_tensor(out=tb[:], in0=t2, in1=sA, op=ALU.mult), reads=[r_qg, r_rope, r_t], writes=[r_t])
            P.op("dve", lambda e: e.tensor_tensor(out=tc2[:], in0=t1, in1=sA, op=ALU.mult), reads=[r_qg, r_rope, r_t], writes=[r_t])
            P.op("dve", lambda e: e.tensor_tensor(out=td[:], in0=t2, in1=cA, op=ALU.mult), reads=[r_qg, r_rope, r_t], writes=[r_t])
            P.op("dve", lambda e: e.tensor_tensor(out=qr[:, :, 0:8], in0=ta[:], in1=tb[:], op=ALU.subtract),
                 reads=[r_t, r_qr], writes=[r_qr])
            P.op("dve", lambda e: e.tensor_tensor(out=qr[:, :, 8:16], in0=tc2[:], in1=td[:], op=ALU.add),
                 reads=[r_t, r_qr], writes=[r_qr])
            P.guard("dve", [r_qr])
            P.op("act", lambda e: e.activation(out=qik[:, 0:8, :].rearrange("p h d -> p (h d)"), in_=PB[:, 128:384],
                                               func=AF.Copy, scale=1.0 / 16.0), reads=[bres[B_PB]], writes=[r_qik])
            P.op("act", lambda e: e.activation(out=qik[:, 8, :], in_=PB[:, 384:416], func=AF.Copy),
                 reads=[bres[B_PB], r_qik], writes=[r_qik])
            P.op("act", lambda e: e.activation(out=witm_all[:, i, :], in_=PB[:, 416:424], func=AF.Copy),
                 reads=[bres[B_PB]], writes=[r_wi])
            cI = cosI[:, i, :].unsqueeze(1).to_broadcast([128, 9, 4])
            sI = sinI[:, i, :].unsqueeze(1).to_broadcast([128, 9, 4])
            u1, u2 = qik[:, :, 0:4], qik[:, :, 4:8]
            P.op("act", lambda e: e.activation(out=qikr[:, 0:9, 8:32], in_=qik[:, :, 8:32], func=AF.Copy), reads=[r_qik], writes=[r_qikr])
            P.op("dve", lambda e: e.tensor_tensor(out=ua[:], in0=u1, in1=cI, op=ALU.mult), reads=[r_qik, r_rope], writes=[r_u])
            P.op("dve", lambda e: e.tensor_tensor(out=ub_[:], in0=u2, in1=sI, op=ALU.mult), reads=[r_qik, r_rope, r_u], writes=[r_u])
            P.op("dve", lambda e: e.tensor_tensor(out=uc[:], in0=u1, in1=sI, op=ALU.mult), reads=[r_qik, r_rope, r_u], writes=[r_u])
            P.op("dve", lambda e: e.tensor_tensor(out=ud[:], in0=u2, in1=cI, op=ALU.mult), reads=[r_qik, r_rope, r_u], writes=[r_u])
            P.op("dve", lambda e: e.tensor_tensor(out=qikr[:, 0:9, 0:4], in0=ua[:], in1=ub_[:], op=ALU.subtract),
                 reads=[r_u, r_qikr], writes=[r_qikr])
            P.op("dve", lambda e: e.tensor_tensor(out=qikr[:, 0:9, 4:8], in0=uc[:], in1=ud[:], op=ALU.add),
                 reads=[r_u, r_qikr], writes=[r_qikr])
            P.op("dve", lambda e: e.tensor_copy(out=qikr[:, 9:12, :], in_=qikr[:, 8:9, :].to_broadcast([128, 3, 32])),
                 reads=[r_qikr], writes=[r_qikr])
            P.guard("dve", [r_qikr])
            qrf = qr[:].rearrange("p h d -> p (h d)")
            for r in range(4):
                P.op("pe", lambda e, r=r: e.transpose(out=t1_ps[:, r * 128:(r + 1) * 128], in_=qrf[:, r * 128:(r + 1) * 128],
                                                      identity=ident), reads=[r_qr, r_cb], writes=[bres[B_T1]])
            P.op("pe", lambda e: e.transpose(out=t1_ps[:, 512:640], in_=qrf[:, 512:640], identity=ident),
                 reads=[r_qr, r_cb], writes=[bres[B_T1]])
            qif = qikr[:].rearrange("p h d -> p (h d)")
            for c in range(3):
                P.op("pe", lambda e, c=c: e.transpose(out=t2_ps[:, c * 128:(c + 1) * 128], in_=qif[:, c * 128:(c + 1) * 128],
                                                      identity=ident), reads=[r_qikr, r_cb], writes=[bres[B_T2]])
            tk = slice(i * 128, (i + 1) * 128)
            P.op("dve", lambda e: e.tensor_copy(out=qT[:, :, tk], in_=t1_ps[:, 0:512].rearrange("p (r t) -> p r t", r=4)),
                 reads=[bres[B_T1]], writes=[r_qT])
            for g_ in range(2):
                P.op("dve", lambda e, g_=g_: e.tensor_scalar(out=kTz[:, g_, tk], in0=t1_ps[:, 512:640], scalar1=cfs("H2", g_, g_ + 1),
                                                             scalar2=None, op0=ALU.mult),
                     reads=[bres[B_T1], r_cf] + ([r_kT] if g_ else []), writes=[r_kT])
            P.op("dve", lambda e: e.tensor_copy(
                out=qiT[:, 2 * i:2 * i + 2, :, :],
                in_=t2_ps[:, 0:256].rearrange("p (h a t) -> p a h t", h=2, a=2)),
                 reads=[bres[B_T2]], writes=[r_qiT])
            P.op("dve", lambda e: e.tensor_copy(out=kiT4[:, tk], in_=t2_ps[:, 256:384]), reads=[bres[B_T2]], writes=[r_kiT])

        n_tiles = NT if "p1_2" not in dbg else 2
        phase1_front(0)
        for i in range(n_tiles):
            if i + 1 < n_tiles:
                phase1_front(i + 1)
            phase1_back(i)
        es1.close()
        P.barrier()
        r_wexp = Res("wexp")
        for hh in range(2):
            P.op("dve", lambda e, hh=hh: e.tensor_tensor(
                out=wexp_all[:].rearrange("p n (a b c) -> p n a b c", a=2, b=2)[:, :, hh, :, :],
                in0=witm_all[:, :, hh * 4:(hh + 1) * 4].unsqueeze(2).to_broadcast([128, NT, 2, 4]),
                in1=cfs("H2").unsqueeze(1).unsqueeze(3).to_broadcast([128, NT, 2, 4]), op=ALU.mult),
                 reads=[r_wi, r_cf, r_wexp], writes=[r_wexp])
        P.op("pe", lambda e: e.matmul(out=banks[0][:], lhsT=cfs("E64"), rhs=wexp_all[:].rearrange("p n c -> p (n c)"),
                                      start=True, stop=True), reads=[r_wexp, r_cf], writes=[bres[0]])
        r_wall = Res("wall")
        for hh in range(2):
            ps = slice(hh * 64, (hh + 1) * 64)
            src = banks[0][ps, :].rearrange("p (n c) -> p n c", c=16)[:, :, hh * 8:hh * 8 + 8]
            P.op("act", lambda e, ps=ps, src=src: e.activation(out=wabs[ps, :, :], in_=src, func=AF.Abs),
                 reads=[bres[0], r_wall], writes=[r_wall])
            P.op("act", lambda e, ps=ps, src=src: e.activation(out=wsgn[ps, :, :], in_=src, func=AF.Sign),
                 reads=[bres[0], r_wall], writes=[r_wall])
        for r_ in r_wabs:
            r_.w = r_wall.w

        es2 = ExitStack()
        scope["es"] = es2
        NSB = 2
        scores = [sb(f"scores{i}", [128, S], F32) for i in range(NSB)]
        r_sc = [Res(f"sc{i}") for i in range(NSB)]
        NMb = [W1_flat[:, i * S:(i + 1) * S] for i in range(NSB)]
        r_NM = [Res(f"NM{i}") for i in range(NSB)]
        Wsel = [sb(f"Wsel{i}", [128, 8, 128], BF16) for i in range(2)]
        r_Wsel = [Res(f"Wsel{i}") for i in range(2)]
        NR = 8
        Rt = [sb(f"Rt{i}", [128, 512], BF16) for i in range(NR)]
        r_R = [Res(f"R{i}") for i in range(NR)]
        NP = 4
        PTt = [sb(f"PTt{i}", [128, 512], BF16) for i in range(NP)]
        r_PT = [Res(f"PT{i}") for i in range(NP)]
        Mx = sb("Mx", [128, 1], F32)
        Mn = sb("Mn", [128, 1], F32)
        Mabs = sb("Mabs", [128, 1], F32)
        bss = sb("bss", [128, NIT], F32)
        bst = sb("bst", [128, NIT], F32)
        mid = [sb(f"mid{i}", [128, 1], F32) for i in range(2)]
        cnt = sb("cnt", [128, 1], F32)
        btmp = sb("btmp", [128, 1], F32)
        bpre = sb("bpre", [128, 1], F32)
        sepscr = sb("sepscr", [128, 32], F32)
        r_bpre = Res("bpre")
        r_bis = Res("bis")
        rdenA = sb("rdenA", [128, 512], F32)
        rdenB = sb("rdenB", [128, 512], F32)
        lnd = sb("lnd", [128, 512], F32)
        r_lnd = Res("lnd")
        bcs = sb("bcs", [128, 512], F32)
        r_rden, r_bcs = Res("rden"), Res("bcs")
        thr_dbg = sb("thr_dbg", [128, NT], F32)

        P.op("dve", lambda e: e.memset(rdenA[:], 0.0), writes=[r_rden])
        P.op("dve", lambda e: e.memset(rdenB[:], 0.0), reads=[r_rden], writes=[r_rden])
        ctr = {"L": 0, "R": 0, "ST": 0, "PT": 0, "relu": 0, "ev": 0, "pool": 0}
        NPB = 5

        def next_pb_idx():
            b = ctr["pool"] % 4
            ctr["pool"] += 1
            return b

        def next_pb():
            b = (5, 7)[ctr["ST"] % 2]
            ctr["ST"] += 1
            return b
        B_IS, B_OT = 4, 6

        def blk_vars(i):
            sbi = i % NSB
            return sbi, i % 2, (i + 1) * 128, i * 128, scores[sbi], NMb[sbi]

        def idx_block(i):
            if i < 2:
                return
            sbi, ws, nk, t0, sc, NM = blk_vars(i)
            nkb = (nk + 511) // 512
            P.op("dve", lambda e: e.tensor_tensor(
                out=Wsel[ws][:].rearrange("p (a b) t -> p a b t", a=2),
                in0=cbs("Dsel2").rearrange("p (a t) -> p a t", a=2).unsqueeze(2).to_broadcast([128, 2, 4, 128]),
                in1=wsgn[:, i, :].rearrange("p (a b) -> p a b", a=2).unsqueeze(3).to_broadcast([128, 2, 4, 128]),
                op=ALU.mult), reads=[r_cb, r_wabs[i]], writes=[r_Wsel[ws]])
            P.guard("dve", [r_Wsel[ws]])

            def kw(kb):
                return min(512, nk - kb * 512)

            grp = [(kb, th) for kb in range(nkb) for th in range(2)]
            lb_of = {}

            def emitL4(n):
                kb, th = grp[n]
                w = kw(kb)
                tq0 = t0 + th * 64
                lbs = [next_pb_idx() for _ in range(4)]
                lb_of[n] = lbs
                for pg in range(4):
                    lb = lbs[pg]
                    P.op("pe", lambda e, pg=pg, lb=lb, w=w, kb=kb, tq0=tq0: e.matmul(
                        out=banks[lb][:, 0:w],
                        lhsT=qiT[pg * 32:(pg + 1) * 32, tq0 // 64, :, :].rearrange("p h t -> p (h t)"),
                        rhs=kiT4[pg * 32:(pg + 1) * 32, kb * 512:kb * 512 + w],
                        start=True, stop=True, tile_position=(pg * 32, 0)),
                         reads=[r_qiT, r_kiT], writes=[bres[lb]])

            emitL4(0)
            for n in range(len(grp)):
                kb, th = grp[n]
                w = kw(kb)
                lbs = lb_of[n]
                rbs = []
                for pg in range(4):
                    j8 = th * 4 + pg
                    lb = lbs[pg]
                    rb = ctr["R"] % NR
                    ctr["R"] += 1
                    rbs.append(rb)
                    wcol = wabs[:, i, j8:j8 + 1]
                    P.op("act", lambda e, lb=lb, rb=rb, wcol=wcol, w=w: e.activation(
                        out=Rt[rb][:, 0:w], in_=banks[lb][:, 0:w], func=AF.Relu, scale=wcol),
                         reads=[bres[lb], r_wabs[i]], writes=[r_R[rb]])
                for pg in range(4):
                    j8 = th * 4 + pg
                    rb = rbs[pg]
                    P.op("pe", lambda e, rb=rb, j8=j8, w=w: e.matmul(out=banks[B_IS][:, 0:w], lhsT=Wsel[ws][:, j8, :],
                                                                rhs=Rt[rb][:, 0:w], start=(j8 == 0), stop=(j8 == 7)),
                         reads=[r_Wsel[ws], r_R[rb]], writes=[bres[B_IS]])
                if n + 1 < len(grp):
                    emitL4(n + 1)
                if th == 1:
                    ks = slice(kb * 512, kb * 512 + w)
                    P.op("act", lambda e, ks=ks, w=w: e.activation(out=sc[:, ks], in_=banks[B_IS][:, 0:w], func=AF.Copy),
                         reads=[bres[B_IS]], writes=[r_sc[sbi]])

        def bis_block(i, fill):
            sbi, ws, nk, t0, sc, NM = blk_vars(i)
            if i < 2:
                if nk > 128:
                    P.op("dve", lambda e: e.memset(NM[:, 0:nk - 128], 0.0), writes=[r_NM[sbi]])
                P.op("dve", lambda e: e.tensor_copy(out=NM[:, nk - 128:nk], in_=cfs("cmaskb")),
                     reads=[r_cf, r_NM[sbi]], writes=[r_NM[sbi]])
                P.guard("dve", [r_NM[sbi]])
                P.play(fill)
                return
            nsl = NIT + 1
            per = (len(fill) + nsl - 1) // nsl
            chunks = [fill[k * per:(k + 1) * per] for k in range(nsl)]
            P.op("dve", lambda e: e.tensor_scalar(out=NM[:, 0:nk], in0=sc[:, 0:nk], scalar1=0.0, scalar2=-3.0e38,
                                                  op0=ALU.add, op1=ALU.max, accum_out=Mx[:]),
                 reads=[r_sc[sbi]], writes=[r_NM[sbi], r_bis])
            P.op("dve", lambda e: e.tensor_scalar(out=NM[:, 0:nk], in0=sc[:, 0:nk], scalar1=0.0, scalar2=3.0e38,
                                                  op0=ALU.add, op1=ALU.min, accum_out=Mn[:]),
                 reads=[r_sc[sbi], r_bis], writes=[r_NM[sbi], r_bis])
            P.op("dve", lambda e: e.memset(mid[1][:], 0.0), reads=[r_bis], writes=[r_bis])
            P.op("dve", lambda e: e.scalar_tensor_tensor(out=Mabs[:], in0=Mn[:], scalar=-1.0, in1=Mx[:],
                                                         op0=ALU.mult, op1=ALU.max), reads=[r_bis], writes=[r_bis])
            P.op("dve", lambda e: e.memset(sepscr[:], 0.0))
            P.op("dve", lambda e: e.tensor_scalar(out=bss[:], in0=cfs("bis_s"), scalar1=Mabs[:, 0:1], scalar2=None,
                                                  op0=ALU.mult), reads=[r_bis, r_cf], writes=[r_bis])
            P.op("dve", lambda e: e.tensor_scalar(out=bst[:], in0=cfs("bis_t"), scalar1=Mabs[:, 0:1], scalar2=None,
                                                  op0=ALU.mult), reads=[r_bis, r_cf], writes=[r_bis])
            P.op("dve", lambda e: e.tensor_tensor(out=sc[:, nk - 128:nk], in0=sc[:, nk - 128:nk], in1=cfs("cmask"),
                                                  op=ALU.add), reads=[r_sc[sbi], r_cf, r_bis], writes=[r_sc[sbi]])
            P.play(chunks[0])

            def sep(k=0):
                for _ in range(k):
                    P.op("dve", lambda e: e.memset(sepscr[:], 0.0))

            for n in range(1, NIT + 1):
                mcur, mnext = mid[n % 2], mid[(n + 1) % 2]
                P.op("dve", lambda e, mcur=mcur: e.tensor_scalar(out=NM[:, 0:nk], in0=sc[:, 0:nk], scalar1=mcur[:, 0:1],
                                                                scalar2=0.0, op0=ALU.is_ge, op1=ALU.add, accum_out=cnt[:]),
                     reads=[r_sc[sbi], r_bis], writes=[r_NM[sbi], r_bis])
                P.op("dve", lambda e, n=n: e.tensor_scalar(out=btmp[:], in0=cnt[:], scalar1=TOPK - 0.5, scalar2=bss[:, n - 1:n],
                                                           op0=ALU.is_ge, op1=ALU.mult), reads=[r_bis], writes=[r_bis])
                P.op("dve", lambda e, n=n, mcur=mcur, mnext=mnext: e.tensor_scalar(
                    out=mnext[:], in0=btmp[:], scalar1=bst[:, n - 1:n], scalar2=mcur[:, 0:1],
                    op0=ALU.add, op1=ALU.add), reads=[r_bis], writes=[r_bis])
                P.play(chunks[n])
            thr = mid[(NIT + 1) % 2]
            if "p2" in dbg:
                P.op("dve", lambda e: e.tensor_copy(out=thr_dbg[:, i:i + 1], in_=thr[:]), reads=[r_bis], writes=[r_bis])
            P.op("dve", lambda e: e.tensor_scalar(out=NM[:, 0:nk], in0=sc[:, 0:nk], scalar1=thr[:, 0:1], scalar2=NEG_MASK,
                                                  op0=ALU.is_lt, op1=ALU.mult), reads=[r_sc[sbi], r_bis], writes=[r_NM[sbi]])
            P.guard("dve", [r_NM[sbi]])

        def att_block(i):
            sbi, ws, nk, t0, sc, NM = blk_vars(i)

            def attn_group(g):
                gp = slice(g * 64, (g + 1) * 64)
                ob = B_OT
                stbank = {}

                def emitQK(j):
                    stb = next_pb()
                    stbank[j] = stb
                    P.op("pe", lambda e: e.matmul(out=banks[stb][:], lhsT=kTz[:, g, j * 128:(j + 1) * 128],
                                                  rhs=qT[:, :, t0:t0 + 128], start=True, stop=False),
                         reads=[r_kT, r_qT], writes=[bres[stb]])
                    P.op("pe", lambda e: e.matmul(out=banks[stb][:], lhsT=NM[:, j * 128:(j + 1) * 128], rhs=cbs("I4"),
                                                  start=False, stop=True),
                         reads=[r_NM[sbi], r_cb], writes=[bres[stb]])

                emitQK(0)
                for j in range(i + 1):
                    stb = stbank[j]
                    pb = ctr["PT"] % NP
                    ctr["PT"] += 1
                    P.op("act", lambda e, stb=stb, pb=pb: e.activation(out=PTt[pb][:], in_=banks[stb][:], func=AF.Exp, scale=0.125),
                         reads=[bres[stb]], writes=[r_PT[pb]])
                    if j + 1 <= i:
                        emitQK(j + 1)
                    lv = v_sb[:, j, 0:128] if g == 0 else v_sb[:, j, 65:193]
                    P.op("pe", lambda e, j=j, pb=pb, lv=lv: e.matmul(out=banks[ob][:, :], lhsT=lv, rhs=PTt[pb][:],
                                                                     start=(j == 0), stop=(j == i)),
                         reads=[r_v, r_PT[pb]], writes=[bres[ob]])
                dp = 64 if g == 0 else 0
                rd = rdenA if g == 0 else rdenB
                P.op("act", lambda e: e.activation(out=lnd[dp:dp + 1, :], in_=banks[ob][dp:dp + 1, :], func=AF.Ln),
                     reads=[bres[ob]], writes=[r_lnd])
                P.op("act", lambda e: e.activation(out=rd[dp:dp + 1, :], in_=lnd[dp:dp + 1, :], func=AF.Exp, scale=-1.0),
                     reads=[r_lnd], writes=[r_rden])
                bcb = next_pb()
                P.op("pe", lambda e: e.matmul(out=banks[bcb][:, :], lhsT=cfs("onesf"), rhs=rd[:, :], start=True, stop=True),
                     reads=[r_rden, r_cf], writes=[bres[bcb]])
                P.op("act", lambda e: e.activation(out=bcs[gp, :], in_=banks[bcb][gp, :], func=AF.Copy),
                     reads=[bres[bcb]], writes=[r_bcs])
                P.op("dve", lambda e: e.tensor_tensor(out=aT[gp, :, t0:t0 + 128],
                                                      in0=banks[ob][gp, :].rearrange("p (r t) -> p r t", r=4),
                                                      in1=bcs[gp, :].rearrange("p (r t) -> p r t", r=4), op=ALU.mult),
                     reads=[bres[ob], r_bcs], writes=[r_aT])

            attn_group(0)
            attn_group(1)

        if "p2f" in dbg:
            p2_blocks = list(range(NT))
        elif "p2" in dbg:
            p2_blocks = [0, 1, 2, 3, 4]
        elif "p1" in dbg or "p1_2" in dbg:
            p2_blocks = []
        else:
            p2_blocks = list(range(NT))
        idx_block(p2_blocks[0]) if p2_blocks else None
        for n_, i in enumerate(p2_blocks):
            P.capture_begin()
            if n_ >= 1:
                att_block(p2_blocks[n_ - 1])
            capA = P.capture_end()
            P.capture_begin()
            if n_ + 1 < len(p2_blocks):
                idx_block(p2_blocks[n_ + 1])
            capI = P.capture_end()
            fill = []
            ia = ii = 0
            na, ni = len(capA), len(capI)
            while ia < na or ii < ni:
                if ii >= ni or (ia < na and ia * max(ni, 1) <= ii * max(na, 1)):
                    fill.append(capA[ia]); ia += 1
                else:
                    fill.append(capI[ii]); ii += 1
            bis_block(i, fill)
        if p2_blocks:
            att_block(p2_blocks[-1])
        es2.close()
        es_p12.close()
        scope["es"] = es
        P.barrier()

        if not ({"p1", "p1_2", "p2", "p2f"} & dbg):
            es3 = ExitStack()
            scope["es"] = es3
            o = 0
            W3 = arena[:, o:o + 8 * 3584].rearrange("p (k c) -> p k c", k=8); o += 8 * 3584
            Wa = arena[:, o:o + 4 * 1024].rearrange("p (k c) -> p k c", k=4); o += 4 * 1024
            Wb = arena[:, o:o + 4 * 1024].rearrange("p (k c) -> p k c", k=4); o += 4 * 1024
            Wo = arena[:, o:o + 8 * 1024].rearrange("p (k c) -> p k c", k=8); o += 8 * 1024
            PW = arena[:, o:o + 4 * 128].rearrange("p (k c) -> p k c", k=4); o += 4 * 128
            r_W3 = [Res(f"W3a_{k}") for k in range(8)]
            r_W3b = [Res(f"W3b_{k}") for k in range(8)]
            r_Wa, r_Wb, r_PW = Res("Wa"), Res("Wb"), Res("PW")
            r_Wo = [Res(f"Wo{k}") for k in range(2)]
            w3v = w3_d.rearrange("(k p) c -> p k c", p=128)
            for kc in range(8):
                P.op("pool", lambda e, kc=kc: e.dma_start(out=W3[:, kc, 0:1536], in_=w3v[:, kc, 0:1536]), writes=[r_W3[kc]], chan=f"d_w3a_{kc}")
            P.op("pool", lambda e: e.dma_start(out=PW, in_=pw_d.rearrange("(k p) c -> p k c", p=128)), writes=[r_PW], chan="d_pw")
            P.op("pool", lambda e: e.dma_start(out=Wa, in_=wa_d.rearrange("(k p) c -> p k c", p=128)), writes=[r_Wa], chan="d_wa")
            P.op("pool", lambda e: e.dma_start(out=Wb, in_=wb_d.rearrange("(k p) c -> p k c", p=128)), writes=[r_Wb], chan="d_wb")
            for kc in range(8):
                P.op("pool", lambda e, kc=kc: e.dma_start(out=W3[:, kc, 1536:3584], in_=w3v[:, kc, 1536:3584]), writes=[r_W3b[kc]], chan=f"d_w3b_{kc}")
            wov = wo_d.rearrange("(k p) c -> p k c", p=128)
            for hk in range(2):
                P.op("pool", lambda e, hk=hk: e.dma_start(out=Wo[:, 4 * hk:4 * hk + 4, :], in_=wov[:, 4 * hk:4 * hk + 4, :]),
                     writes=[r_Wo[hk]], chan=f"d_wo{hk}")

            xblk = sb("xblk", [128, 4, D], F32)
            r_xb = [Res(f"xb{q}") for q in range(4)]
            xsq3 = sb("xsq3", [128, D], BF16)
            ssx3 = sb("ssx3", [128, 1], F32)
            rsx3 = sb("rsx3", [128, 1], F32)
            xn3s = [sb(f"xn3_{k}", [128, D], BF16) for k in range(2)]
            r_xn3s = [Res(f"xn3_{k}") for k in range(2)]
            r_ssx3, r_xsq3 = Res("ssx3"), Res("xsq3")
            hTb = sb("hTb", [128, 8, 512], BF16)
            r_hTb = Res("hTb")
            zs2 = [arena[:, o + k * 512:o + (k + 1) * 512] for k in range(2)]
            r_zs2 = [Res(f"zs{k}") for k in range(2)]
            agT = sb("agT", [128, 4, 512], BF16)
            r_ag = Res("ag")
            uT = sb("uT", [128, 4, 528], F32)
            r_u3 = [Res(f"u3_{g}") for g in range(4)]
            pa = sb("pa", [128, 528], F32)
            pb2 = sb("pb2", [128, 528], F32)
            r_pp = Res("pp")
            tmp16 = sb("tmp16", [128, 16], F32)
            pooledT = sb("pooledT", [128, 4, 512], BF16)
            r_pool = [Res(f"pool{g}") for g in range(4)]
            zbs = sb("zbs", [128, 512], BF16)
            r_zbs = Res("zbs")
            bT = sb("bT", [128, 4, 512], BF16)
            r_bT = Res("bT")
            sgA2 = [arena[:, o + 1024:o + 1536], sb("sgA1", [128, 512], BF16)]
            sgB_ = sb("sgB0", [128, 512], BF16)
            sgB2 = [sgB_, sgB_]
            r_sgA2 = [Res(f"sgA{k}") for k in range(2)]
            r_sgB_ = Res("sgB")
            r_sgB2 = [r_sgB_, r_sgB_]
            tA = sb("tA", [128, 512], F32)
            tB = sb("tB", [128, 512], F32)
            r_tA, r_tB = Res("tA"), Res("tB")
            yT = arena[:, o + 1536:o + 1536 + 8 * 512].rearrange("p (k t) -> p k t", k=8)
            r_yT = Res("yT")
            ot = sb("ot", [128, D], F32)
            r_ot = Res("ot")
            pj = {"n": 0, "o": 0}
            xT3_ps = banks[0][:].bitcast(BF16)
            out_toks = []

            def next_pj():
                b = 1 + pj["n"] % 7
                pj["n"] += 1
                return b

            def proj_fm(col0, reads_extra=()):
                b = next_pj()
                for kc in range(8):
                    P.op("pe", lambda e, kc=kc, b=b: e.matmul(out=banks[b][:], lhsT=W3[:, kc, col0:col0 + 128], rhs=hTb[:, kc, :],
                                                              start=(kc == 0), stop=(kc == 7)),
                         reads=[(r_W3[kc] if col0 < 1536 else r_W3b[kc]), r_hTb], writes=[bres[b]])
                return b

            def phase3_block(tb):
                tok0 = tb * 512
                tsl = slice(tok0, tok0 + 512)
                for q in range(4):
                    tile = 4 * tb + q
                    xa = xblk[:, q, :]
                    xn3, r_xn3 = xn3s[q % 2], r_xn3s[q % 2]
                    P.op("pool", lambda e, xa=xa, tile=tile: e.dma_start(out=xa, in_=x[tile * 128:(tile + 1) * 128, :]),
                         writes=[r_xb[q]], chan=f"d_xb{q}")
                    P.op("act", lambda e, xa=xa: e.activation(out=xsq3[:], in_=xa, func=AF.Square, accum_out=ssx3[:]),
                         reads=[r_xb[q]], writes=[r_xsq3, r_ssx3])
                    P.op("act", lambda e: e.activation(out=rsx3[:], in_=ssx3[:], func=AF.Sqrt, scale=1.0 / D, bias=cfs("eps")),
                         reads=[r_ssx3, r_cf], writes=[r_ssx3])
                    P.op("dve", lambda e: e.reciprocal(out=ssx3[:], in_=rsx3[:]), reads=[r_ssx3], writes=[r_ssx3])
                    P.op("dve", lambda e, xa=xa, xn3=xn3: e.scalar_tensor_tensor(out=xn3[:], in0=xa, scalar=ssx3[:], in1=cfs("gnorm"),
                                                                        op0=ALU.mult, op1=ALU.mult),
                         reads=[r_xb[q], r_ssx3, r_cf], writes=[r_xn3])
                    P.guard("dve", [r_xn3])
                    for kc in range(8):
                        P.op("pe", lambda e, kc=kc, xn3=xn3: e.transpose(out=xT3_ps[:, kc * 128:(kc + 1) * 128],
                                                                in_=xn3[:, kc * 128:(kc + 1) * 128], identity=ident),
                             reads=[r_xn3, r_cb], writes=[bres[0]])
                    P.op("dve", lambda e, q=q: e.tensor_copy(out=hTb[:, :, q * 128:(q + 1) * 128],
                                                             in_=xT3_ps.rearrange("p (k t) -> p k t", k=8)),
                         reads=[bres[0]], writes=[r_hTb])
                for c in range(4):
                    b = proj_fm(c * 128)
                    zs, r_zs = zs2[c % 2], r_zs2[c % 2]
                    P.op("act", lambda e, b=b, zs=zs: e.activation(out=zs[:], in_=banks[b][:], func=AF.Silu),
                         reads=[bres[b]], writes=[r_zs])
                    P.op("dve", lambda e, c=c, zs=zs: e.tensor_tensor(out=agT[:, c, :], in0=zs[:], in1=aT[:, c, tsl], op=ALU.mult),
                         reads=[r_zs, r_aT], writes=[r_ag])
                if tb == 0:
                    P.op("dve", lambda e: e.memset(uT[:, :, 0:16], 0.0), writes=r_u3)
                ub_bank = {0: proj_fm(512)}
                for g in range(4):
                    b = ub_bank[g]
                    P.op("act", lambda e, b=b, g=g: e.activation(out=uT[:, g, 16:528], in_=banks[b][:], func=AF.Copy),
                         reads=[bres[b]], writes=[r_u3[g]])
                    U = uT[:, g, :]
                    wdw = 2 ** (g + 1)
                    P.op("dve", lambda e, U=U: e.tensor_tensor(out=pa[:, 1:528], in0=U[:, 1:528], in1=U[:, 0:527], op=ALU.add),
                         reads=[r_u3[g]], writes=[r_pp])
                    last = pa
                    if g >= 1:
                        P.op("dve", lambda e: e.tensor_tensor(out=pb2[:, 3:528], in0=pa[:, 3:528], in1=pa[:, 1:526], op=ALU.add),
                             reads=[r_pp], writes=[r_pp])
                        last = pb2
                    if g >= 2:
                        P.op("dve", lambda e: e.tensor_tensor(out=pa[:, 7:528], in0=pb2[:, 7:528], in1=pb2[:, 3:524], op=ALU.add),
                             reads=[r_pp], writes=[r_pp])
                        last = pa
                    if g >= 3:
                        P.op("dve", lambda e: e.tensor_tensor(out=pb2[:, 15:528], in0=pa[:, 15:528], in1=pa[:, 7:520], op=ALU.add),
                             reads=[r_pp], writes=[r_pp])
                        last = pb2
                    P.op("dve", lambda e, g=g, U=U, last=last, wdw=wdw: e.scalar_tensor_tensor(
                        out=pooledT[:, g, :], in0=last[:, 16:528], scalar=1.0 / wdw, in1=U[:, 16:528],
                        op0=ALU.mult, op1=ALU.subtract), reads=[r_pp, r_u3[g]], writes=[r_pool[g]])
                    if tb == 0:
                        P.op("dve", lambda e, g=g, last=last: e.tensor_tensor(out=tmp16[:], in0=last[:, 16:32],
                                                                              in1=cfs("invcnt", g * 16, g * 16 + 16), op=ALU.mult),
                             reads=[r_pp, r_cf], writes=[r_pp])
                        P.op("dve", lambda e, g=g, U=U: e.tensor_tensor(out=pooledT[:, g, 0:16], in0=tmp16[:], in1=U[:, 16:32],
                                                                        op=ALU.subtract),
                             reads=[r_pp, r_u3[g], r_pool[g]], writes=[r_pool[g]])
                    P.op("dve", lambda e, g=g: e.tensor_copy(out=uT[:, g, 0:16], in_=uT[:, g, 512:528]),
                         reads=[r_u3[g], r_pp], writes=[r_u3[g]])
                    if g + 1 < 4:
                        ub_bank[g + 1] = proj_fm(512 + (g + 1) * 128)
                    bz = proj_fm(1024 + g * 128)
                    P.op("act", lambda e, bz=bz: e.activation(out=zbs[:], in_=banks[bz][:], func=AF.Silu),
                         reads=[bres[bz]], writes=[r_zbs])
                    bm = next_pj()
                    P.op("pe", lambda e, g=g, bm=bm: e.matmul(out=banks[bm][:], lhsT=PW[:, g, :], rhs=pooledT[:, g, :],
                                                              start=True, stop=True),
                         reads=[r_PW, r_pool[g]], writes=[bres[bm]])
                    P.op("dve", lambda e, g=g, bm=bm: e.scalar_tensor_tensor(
                        out=bT[:, g, :], in0=banks[bm][:], scalar=cfs("pscale", g, g + 1), in1=zbs[:],
                        op0=ALU.mult, op1=ALU.mult), reads=[bres[bm], r_zbs, r_cf], writes=[r_bT])
                for m in range(8):
                    msl = slice(m * 128, (m + 1) * 128)
                    sgA, sgB, r_sgA, r_sgB = sgA2[m % 2], sgB2[m % 2], r_sgA2[m % 2], r_sgB2[m % 2]
                    bA = next_pj()
                    for c in range(4):
                        P.op("pe", lambda e, c=c, bA=bA, msl=msl: e.matmul(out=banks[bA][:], lhsT=Wa[:, c, msl], rhs=agT[:, c, :],
                                                                           start=(c == 0), stop=(c == 3)),
                             reads=[r_Wa, r_ag], writes=[bres[bA]])
                    bgA = proj_fm(1536 + m * 128)
                    P.op("act", lambda e, bgA=bgA, m=m, sgA=sgA: e.activation(out=sgA[:], in_=banks[bgA][:], func=AF.Sigmoid,
                                                                     bias=cfs("mbias", m, m + 1)),
                         reads=[bres[bgA], r_cf], writes=[r_sgA])
                    P.op("dve", lambda e, bA=bA, sgA=sgA: e.tensor_tensor(out=tA[:], in0=banks[bA][:], in1=sgA[:], op=ALU.mult),
                         reads=[bres[bA], r_sgA], writes=[r_tA])
                    bB = next_pj()
                    for c in range(4):
                        P.op("pe", lambda e, c=c, bB=bB, msl=msl: e.matmul(out=banks[bB][:], lhsT=Wb[:, c, msl], rhs=bT[:, c, :],
                                                                           start=(c == 0), stop=(c == 3)),
                             reads=[r_Wb, r_bT], writes=[bres[bB]])
                    bgB = proj_fm(1536 + 1024 + m * 128)
                    P.op("act", lambda e, bgB=bgB, m=m, sgB=sgB: e.activation(out=sgB[:], in_=banks[bgB][:], func=AF.Sigmoid,
                                                                     bias=cfs("mbias", 8 + m, 9 + m)),
                         reads=[bres[bgB], r_cf], writes=[r_sgB])
                    P.op("dve", lambda e, bB=bB, sgB=sgB: e.tensor_tensor(out=tB[:], in0=banks[bB][:], in1=sgB[:], op=ALU.mult),
                         reads=[bres[bB], r_sgB], writes=[r_tB])
                    P.op("dve", lambda e, m=m: e.tensor_tensor(out=yT[:, m, :], in0=tA[:], in1=tB[:], op=ALU.add),
                         reads=[r_tA, r_tB], writes=[r_yT])
                    if m == 7:
                        P.guard("dve", [r_yT])
                for q in range(4):
                    for half in range(2):
                        bo = 5 + pj["o"] % 3
                        pj["o"] += 1
                        hs_ = slice(half * 512, (half + 1) * 512)
                        for kc in range(8):
                            P.op("pe", lambda e, kc=kc, bo=bo, q=q, hs_=hs_: e.matmul(
                                out=banks[bo][:], lhsT=yT[:, kc, q * 128:(q + 1) * 128], rhs=Wo[:, kc, hs_],
                                start=(kc == 0), stop=(kc == 7)),
                                 reads=[r_yT, r_Wo[kc // 4]], writes=[bres[bo]])
                        P.op("dve", lambda e, bo=bo, q=q, hs_=hs_: e.tensor_tensor(out=ot[:, hs_], in0=banks[bo][:], in1=xblk[:, q, hs_],
                                                                                   op=ALU.add),
                             reads=[bres[bo], r_xb[q]], writes=[r_ot])
                    tile = 4 * tb + q
                    out_toks.append(P.op("sp", lambda e, tile=tile: e.dma_start(out=out_d[tile * 128:(tile + 1) * 128, :], in_=ot[:]),
                                         reads=[r_ot], chan="d_out"))

            for tbi in range(8):
                phase3_block(tbi)
            es3.close()
            scope["es"] = es

        finals = []
        if not ({"p1", "p1_2", "p2", "p2f"} & dbg):
            finals.append(out_toks[-1])
        if "p1" in dbg or "p1_2" in dbg:
            def dump(name, ap, res, dt=F32, shape=None):
                d = dout(name, shape or list(ap.shape), dt)
                finals.append(P.op("sp", lambda e: e.dma_start(out=d, in_=ap), reads=res, chan="d_dbg_" + name))
            dump("d_cosA", cosA[:], [r_rope])
            dump("d_sinA", sinA[:], [r_rope])
            dump("d_cosI", cosI[:], [r_rope])
            dump("d_sinI", sinI[:], [r_rope])
            dump("d_qT", qT, [r_qT], BF16, [128, 4, S])
            dump("d_kT", kTz, [r_kT], BF16, [128, 2, S])
            dump("d_qiT", qiT, [r_qiT], BF16, [128, 64, 2, 64])
            dump("d_kiT4", kiT4, [r_kiT], BF16, [128, S])
            dump("d_v", v_sb, [r_v], BF16, [128, NT, 193])
            dump("d_wabs", wabs[:], r_wabs)
            dump("d_wsgn", wsgn[:], r_wabs)
        if "p2" in dbg:
            def dump2(name, ap, res, dt=F32, shape=None):
                d = dout(name, shape or list(ap.shape), dt)
                finals.append(P.op("sp", lambda e: e.dma_start(out=d, in_=ap), reads=res, chan="d_dbg_" + name))
            dump2("d_aT", aT[:], [r_aT], BF16)
            dump2("d_thr", thr_dbg[:], [r_bis])
            dump2("d_sc", scores[4 % NSB][:], [r_sc[4 % NSB]])
            dump2("d_NM", NMb[4 % NSB][:], [r_NM[4 % NSB]], BF16)
        if "p2f" in dbg:
            d_ = dout("d_thr", [128, NT], F32)
            finals.append(P.op("sp", lambda e: e.dma_start(out=d_, in_=thr_dbg[:]), reads=[r_bis], chan="d_dbg_thr"))
            d2_ = dout("d_aT", [128, 4, S], BF16)
            finals.append(P.op("sp", lambda e: e.dma_start(out=d2_, in_=aT[:]), reads=[r_aT], chan="d_dbg_aT"))
            d5_ = dout("d_qiT", [128, 2 * S], BF16)
            finals.append(P.op("sp", lambda e: e.dma_start(out=d5_, in_=qiT.rearrange("p b h t -> p (b h t)")), reads=[r_qiT], chan="d_dbg_qiT"))
            d6_ = dout("d_wabs", [128, NT, 8], F32)
            finals.append(P.op("sp", lambda e: e.dma_start(out=d6_, in_=wabs[:]), reads=r_wabs, chan="d_dbg_wabs"))
            d7_ = dout("d_wsgn", [128, NT, 8], F32)
            finals.append(P.op("sp", lambda e: e.dma_start(out=d7_, in_=wsgn[:]), reads=r_wabs, chan="d_dbg_wsgn"))
            d3_ = dout("d_sc", [128, S], F32)
            finals.append(P.op("sp", lambda e: e.dma_start(out=d3_, in_=scores[1][:]), reads=[r_sc[1]], chan="d_dbg_sc"))
            d4_ = dout("d_NM", [128, S], BF16)
            finals.append(P.op("sp", lambda e: e.dma_start(out=d4_, in_=NMb[1]), reads=[r_NM[1]], chan="d_dbg_NM"))
        P.final_wait("sp", finals)
        P.emit()
    return nc, list(dbg_d.keys())


def host_inputs(inputs):
    f = np.float32
    w_in = np.asarray(inputs["w_in"], f)[0]
    pq = _perm_q()
    sp = np.cumsum([512, 128, 128, 256, 32, 8, 512, 512, 512, 2048])
    q, k, v, qi, ki, wi, za, ub, zb, gates = np.split(w_in, sp[:-1], axis=1)
    w1 = np.ascontiguousarray(np.concatenate([q[:, pq], k, qi, ki, wi, v], axis=1))
    w3 = np.ascontiguousarray(np.concatenate([za[:, pq], ub, zb, gates], axis=1))
    wa = np.ascontiguousarray(np.asarray(inputs["w_branch_a"], f)[0][pq, :])
    wb = np.ascontiguousarray(np.asarray(inputs["w_branch_b"], f)[0])
    wo = np.ascontiguousarray(np.asarray(inputs["w_out"], f)[0])
    pw = np.ascontiguousarray(np.asarray(inputs["pool_w"], f)[0].reshape(512, 128))
    cf, cb = host_consts()
    o, w = CF["gnorm"]; cf[:, o:o + w] = np.asarray(inputs["norm_g"], f)[0][None, :]
    o, w = CF["gq"]; cf[:, o:o + w] = np.asarray(inputs["q_norm_g"], f)[0][None, :]
    o, w = CF["gk"]; cf[:, o:o + w] = np.asarray(inputs["k_norm_g"], f)[0][None, :]
    mb = np.asarray(inputs["merge_bias"], f)[0]
    o, w = CF["mbias"]; cf[:, o:o + w] = mb.reshape(2, 8, 128).transpose(2, 0, 1).reshape(128, 16)
    ps = np.asarray(inputs["pool_scale"], f)[0]
    o, w = CF["pscale"]; cf[:, o:o + w] = ps.reshape(4, 128).T
    xs = np.asarray(inputs["x"], f)
    pos = np.asarray(inputs["positions"], np.int32)
    maps = []
    for b in range(8):
        maps.append({
            "x": np.ascontiguousarray(xs[b]),
            "posT": np.ascontiguousarray(pos[b].reshape(NT, 128).T),
            "cf": cf, "cb": cb, "w1": w1, "w3": w3, "wa": wa, "wb": wb, "wo": wo, "pw": pw,
        })
    return maps


def kernel(**inputs):
    maps = host_inputs(inputs)
    nc, _ = build()
    res = run_bass_kernel_spmd(nc, maps, core_ids=list(range(8)))
    return np.stack([np.asarray(r["out"], np.float32) for r in res.results], axis=0)
```

```python
import numpy as np
from contextlib import ExitStack
import concourse.bass as bass
import concourse.mybir as mybir
from concourse.bass_utils import run_bass_kernel_spmd

F32 = mybir.dt.float32
BF16 = mybir.dt.bfloat16
I32 = mybir.dt.int32
ALU = mybir.AluOpType
AF = mybir.ActivationFunctionType
AX = mybir.AxisListType

S = 4096
D = 1024
NT = 32
NIT = 14
TOPK = 256
EPS = 1e-6
NEG_MASK = -30000.0
TWO_PI = 2.0 * np.pi
CW1 = 6.28125
CW2 = float(np.float32(TWO_PI - 6.28125))
MAGIC = 12582912.0

CF = {}
_off = 0
for _n, _w in [("gnorm", 1024), ("gq", 64), ("gk", 64), ("invfa", 8), ("invfi", 4),
               ("H2", 2), ("cmask", 128), ("invcnt", 64), ("bis_s", NIT), ("bis_t", NIT),
               ("mbias", 16), ("pscale", 4), ("E64", 128), ("halfpi", 1), ("zero", 1), ("eps", 1), ("onesf", 128), ("cmaskb", 128)]:
    CF[_n] = (_off, _w)
    _off += _w
NCF = _off
CB = {}
_off = 0
for _n, _w in [("ident", 128), ("I4", 512), ("Dsel2", 256), ("ones", 128)]:
    CB[_n] = (_off, _w)
    _off += _w
NCB = _off


class Res:
    __slots__ = ("name", "w", "r")

    def __init__(self, name):
        self.name = name
        self.w = None
        self.r = []


class Prog:
    COMPUTE = ("pe", "act", "dve", "pool")

    def __init__(self, nc, es):
        self.nc = nc
        self.es = es
        self.q = {e: [] for e in ("pe", "act", "dve", "pool", "sp")}
        self.nops = {}
        self.waited = {e: {} for e in self.q}
        self.needed = {}
        self.sems = {}

    def _deps(self, eng, reads, writes):
        need = {}
        for r in reads:
            if r.w is not None:
                k, v = r.w
                need[k] = max(need.get(k, 0), v)
        for w in writes:
            if w.w is not None:
                k, v = w.w
                if not (k == eng and eng == "pe"):
                    need[k] = max(need.get(k, 0), v)
            for (k, v) in w.r:
                need[k] = max(need.get(k, 0), v)
        out = []
        for k, v in need.items():
            if k == eng and eng == "pe":
                continue
            if self.waited[eng].get(k, 0) >= v:
                continue
            self.waited[eng][k] = v
            out.append((k, v))
            self.needed.setdefault(k, set()).add(v)
        return out

    def capture_begin(self):
        self.cap = []

    def capture_end(self):
        c, self.cap = self.cap, None
        return c

    def play(self, items):
        for it in items:
            self.op(*it)

    def op(self, eng, fn, reads=(), writes=(), chan=None):
        if getattr(self, "cap", None) is not None:
            self.cap.append((eng, fn, list(reads), list(writes), chan))
            return None
        waits = self._deps(eng, reads, writes)
        key = chan if chan is not None else eng
        self.nops[key] = self.nops.get(key, 0) + 1
        tok = (key, self.nops[key])
        self.q[eng].append((waits, fn, tok))
        for r in reads:
            r.r.append(tok)
        for w in writes:
            w.w = tok
            w.r = []
        return tok

    def barrier(self):
        latest = [(k, n) for k, n in self.nops.items()]
        for eng in self.q:
            waits = []
            for (k, v) in latest:
                if k == eng:
                    continue
                if self.waited[eng].get(k, 0) >= v:
                    continue
                self.waited[eng][k] = v
                waits.append((k, v))
                self.needed.setdefault(k, set()).add(v)
            if waits:
                self.q[eng].append((waits, None, None))

    def guard(self, eng, writes):
        return None

    def final_wait(self, eng, toks):
        waits = []
        for (k, v) in toks:
            waits.append((k, v))
            self.needed.setdefault(k, set()).add(v)
        self.q[eng].append((waits, None, None))

    def emit(self):
        nc = self.nc
        keys = set(self.nops.keys())
        for k in sorted(keys):
            self.sems[k] = self.es.enter_context(nc.semaphore("s_" + k))
        rank = {}
        for k in keys:
            if k in self.COMPUTE:
                vals = sorted(self.needed.get(k, ()))
                rank[k] = {v: i + 1 for i, v in enumerate(vals)}
        sems = self.sems

        def val(k, v):
            if k in self.COMPUTE:
                return rank[k][v]
            return 16 * v

        def replay(name):
            def body(e):
                for waits, fn, tok in self.q[name]:
                    for (k, v) in waits:
                        e.wait_ge(sems[k], val(k, v))
                    if fn is None:
                        continue
                    inst = fn(e)
                    k, v = tok
                    if k in self.COMPUTE:
                        if v in rank[k]:
                            inst.then_inc(sems[k], 1)
                    else:
                        inst.then_inc(sems[k], 16)
            return body

        with nc.Block() as block:
            block.sync(replay("sp"))
            block.tensor(replay("pe"))
            block.scalar(replay("act"))
            block.vector(replay("dve"))
            block.gpsimd(replay("pool"))


def _perm_q():
    idx = []
    for r in range(4):
        for g in range(2):
            for d in range(64):
                idx.append((g * 4 + r) * 64 + d)
    return np.array(idx, dtype=np.int64)


def host_consts():
    cf = np.zeros((128, NCF), np.float32)
    cb = np.zeros((128, NCB), np.float32)
    p = np.arange(128)

    def put(name, arr):
        o, w = CF[name]
        cf[:, o:o + w] = arr

    half = 8
    invfa = (np.float32(500000.0) ** (-(np.arange(half, dtype=np.float32) / np.float32(half)))).astype(np.float32)
    half = 4
    invfi = (np.float32(500000.0) ** (-(np.arange(half, dtype=np.float32) / np.float32(half)))).astype(np.float32)
    put("invfa", invfa[None, :])
    put("invfi", invfi[None, :])
    put("H2", (p[:, None] // 64 == np.arange(2)[None, :]).astype(np.float32))
    cm = np.zeros((128, 128), np.float32)
    cm[:64, 64:] = -1e30
    put("cmask", cm)
    ic = np.zeros((4, 16), np.float32)
    for g, w in enumerate((2, 4, 8, 16)):
        ic[g] = 1.0 / np.minimum(np.arange(16) + 1, w)
    put("invcnt", ic.reshape(1, 64))
    n = np.arange(1, NIT + 1, dtype=np.float64)
    step = 2.0 * 2.0 ** (-n)
    bs = step.copy()
    bt = np.empty(NIT)
    bt[:-1] = -step[1:]
    bt[-1] = -step[-1]
    put("bis_s", bs[None, :].astype(np.float32))
    put("bis_t", bt[None, :].astype(np.float32))
    put("E64", (p[:, None] % 64 == (np.arange(128)[None, :] % 64)).astype(np.float32))
    put("halfpi", np.float32(np.pi / 2))
    put("zero", 0.0)
    put("eps", np.float32(EPS))
    sel = np.zeros((128, 128), np.float32)
    sel[0, :] = 1.0
    sel[64, :] = 1.0
    put("onesf", sel)
    put("cmaskb", np.where(cm < -1e29, NEG_MASK, 0.0).astype(np.float32))

    def putb(name, arr):
        o, w = CB[name]
        cb[:, o:o + w] = arr

    putb("ident", np.eye(128, dtype=np.float32))
    putb("I4", np.tile(np.eye(128, dtype=np.float32), (1, 4)))
    d2 = np.zeros((128, 2, 128), np.float32)
    for m in range(128):
        tl = m % 64
        for th in range(2):
            d2[m, th, th * 64 + tl] = 1.0
    putb("Dsel2", d2.reshape(128, 256))
    putb("ones", 1.0)
    return cf, cb


def build(dbg=()):
    dbg = set(dbg)
    nc = bass.Bass("TRN2", target_bir_lowering=False)

    def din(name, shape, dt=F32):
        return nc.dram_tensor(name, list(shape), dt, kind="ExternalInput").ap()

    x = din("x", [S, D])
    posT = din("posT", [128, NT], I32)
    cf_d = din("cf", [128, NCF])
    cb_d = din("cb", [128, NCB])
    w1_d = din("w1", [D, 1064])
    w3_d = din("w3", [D, 3584])
    wa_d = din("wa", [512, D])
    wb_d = din("wb", [512, D])
    wo_d = din("wo", [D, D])
    pw_d = din("pw", [512, 128])
    out_d = nc.dram_tensor("out", [S, D], F32, kind="ExternalOutput").ap()
    dbg_d = {}

    def dout(name, shape, dt=F32):
        dbg_d[name] = nc.dram_tensor(name, list(shape), dt, kind="ExternalOutput").ap()
        return dbg_d[name]

    es = ExitStack()
    with es:
        P = Prog(nc, es)

        scope = {"es": es}

        def sb(name, shape, dt):
            return scope["es"].enter_context(nc.sbuf_tensor("sb_" + name, list(shape), dt))

        banks = [es.enter_context(nc.psum_tensor(f"bank{i}", [128, 512], F32)) for i in range(8)]
        bres = [Res(f"bank{i}") for i in range(8)]

        cf = sb("cf", [128, NCF], F32)
        cb = sb("cb", [128, NCB], BF16)
        r_cf, r_cb = Res("cf"), Res("cb")
        P.op("sp", lambda e: e.dma_start(out=cf[:], in_=cf_d), writes=[r_cf], chan="d_cf")
        P.op("pool", lambda e: e.dma_start(out=cb[:], in_=cb_d), writes=[r_cb], chan="d_cb")

        def cfs(name, a=0, b=None):
            o, w = CF[name]
            b = w if b is None else b
            return cf[:, o + a:o + b]

        def cbs(name, a=0, b=None):
            o, w = CB[name]
            b = w if b is None else b
            return cb[:, o + a:o + b]

        ident = cbs("ident")
        gscr_d = sb("gscr_d", [128, 8], F32)
        gscr_a = sb("gscr_a", [128, 8], F32)
        gscr_a2 = sb("gscr_a2", [128, 8], F32)
        P.op("act", lambda e: e.activation(out=gscr_a2[:], in_=cfs("zero").to_broadcast([128, 8]), func=AF.Copy), reads=[r_cf])
        P.guard_fn = {"dve": lambda e: e.memset(gscr_d[:], 0.0),
                      "act": lambda e: e.activation(out=gscr_a[:], in_=gscr_a2[:], func=AF.Copy)}

        A1 = 4 * 4096 + 2 * 4096 + 2 * 4096 + 4096 + NT * 193 + 8 * 1064
        arena = sb("arena", [128, max(A1, 8 * 3584 + 2 * 4 * 1024 + 8 * 1024 + 4 * 128)], BF16)
        o = 0
        qT = arena[:, o:o + 4 * S].rearrange("p (r t) -> p r t", r=4); o += 4 * S
        kTz = arena[:, o:o + 2 * S].rearrange("p (g t) -> p g t", g=2); o += 2 * S
        qiT = arena[:, o:o + 2 * S].rearrange("p (b h t) -> p b h t", h=2, t=64); o += 2 * S
        kiT4 = arena[:, o:o + S]; o += S
        v_sb = arena[:, o:o + NT * 193].rearrange("p (n c) -> p n c", n=NT); o += NT * 193
        W1_flat = arena[:, o:o + 8 * 1064]
        W1 = W1_flat.rearrange("p (k c) -> p k c", k=8); o += 8 * 1064
        r_qT, r_kT, r_qiT, r_kiT, r_v, r_W1 = (Res(n) for n in ("qT", "kT", "qiT", "kiT", "v", "W1"))
        ph12 = [r_qT, r_kT, r_qiT, r_kiT, r_v]

        aT = sb("aT", [128, 4, S], BF16)
        r_aT = Res("aT")
        es_p12 = ExitStack()
        scope["es"] = es_p12
        witm_all = sb("witm_all", [128, NT, 8], F32)
        wexp_all = sb("wexp_all", [128, NT, 16], F32)
        wabs = sb("wabs", [128, NT, 8], F32)
        wsgn = sb("wsgn", [128, NT, 8], F32)
        r_wabs = [Res(f"wabs{i}") for i in range(NT)]
        cosA = sb("cosA", [128, NT, 8], F32)
        sinA = sb("sinA", [128, NT, 8], F32)
        cosI = sb("cosI", [128, NT, 4], F32)
        sinI = sb("sinI", [128, NT, 4], F32)
        r_rope = Res("rope")

        w1v = w1_d.rearrange("(k p) c -> p k c", p=128)
        r_W1c = [Res(f"W1c{j}") for j in range(4)]
        for kc in range(0, 8, 2):
            P.op("pool", lambda e, kc=kc: e.dma_start(out=W1[:, kc:kc + 2, :], in_=w1v[:, kc:kc + 2, :]),
                 writes=[r_W1c[kc // 2]], chan=f"d_w1_{kc}")

        es0 = ExitStack()
        scope["es"] = es0
        pos_i = sb("pos_i", [128, NT], I32)
        pos_f = sb("pos_f", [128, NT], F32)
        r_pos = Res("pos")
        P.op("sp", lambda e: e.dma_start(out=pos_i[:], in_=posT), writes=[r_pos], chan="d_pos")
        P.op("dve", lambda e: e.tensor_copy(out=pos_f[:], in_=pos_i[:]), reads=[r_pos], writes=[r_pos])
        ang = sb("ang", [128, NT, 12], F32)
        tq = sb("tq", [128, NT, 12], F32)
        nq = sb("nq", [128, NT, 12], F32)
        rq = sb("rq", [128, NT, 12], F32)
        aq = sb("aq", [128, NT, 12], F32)
        r_ang = Res("ang")
        P.op("dve", lambda e: e.tensor_tensor(out=ang[:, :, 0:8], in0=pos_f[:].unsqueeze(2).to_broadcast([128, NT, 8]),
                                              in1=cfs("invfa").unsqueeze(1).to_broadcast([128, NT, 8]), op=ALU.mult),
             reads=[r_pos, r_cf], writes=[r_ang])
        P.op("dve", lambda e: e.tensor_tensor(out=ang[:, :, 8:12], in0=pos_f[:].unsqueeze(2).to_broadcast([128, NT, 4]),
                                              in1=cfs("invfi").unsqueeze(1).to_broadcast([128, NT, 4]), op=ALU.mult),
             reads=[r_pos, r_cf, r_ang], writes=[r_ang])
        P.op("dve", lambda e: e.tensor_scalar(out=tq[:], in0=ang[:], scalar1=float(1.0 / TWO_PI), scalar2=MAGIC,
                                              op0=ALU.mult, op1=ALU.add), reads=[r_ang], writes=[r_ang])
        P.op("dve", lambda e: e.tensor_scalar(out=nq[:], in0=tq[:], scalar1=-MAGIC, scalar2=None, op0=ALU.add),
             reads=[r_ang], writes=[r_ang])
        P.op("dve", lambda e: e.scalar_tensor_tensor(out=rq[:], in0=nq[:], scalar=-CW1, in1=ang[:],
                                                     op0=ALU.mult, op1=ALU.add), reads=[r_ang], writes=[r_ang])
        P.op("dve", lambda e: e.scalar_tensor_tensor(out=tq[:], in0=nq[:], scalar=-CW2, in1=rq[:],
                                                     op0=ALU.mult, op1=ALU.add), reads=[r_ang], writes=[r_ang])
        P.op("dve", lambda e: e.tensor_scalar(out=rq[:], in0=tq[:], scalar1=3.1415925, scalar2=-3.1415925,
                                              op0=ALU.min, op1=ALU.max), reads=[r_ang], writes=[r_ang])
        P.op("act", lambda e: e.activation(out=aq[:], in_=rq[:], func=AF.Abs), reads=[r_ang], writes=[r_ang])
        P.op("act", lambda e: e.activation(out=sinA[:], in_=rq[:, :, 0:8], func=AF.Sin), reads=[r_ang], writes=[r_rope])
        P.op("act", lambda e: e.activation(out=sinI[:], in_=rq[:, :, 8:12], func=AF.Sin), reads=[r_ang, r_rope], writes=[r_rope])
        P.op("act", lambda e: e.activation(out=cosA[:], in_=aq[:, :, 0:8], func=AF.Sin, scale=-1.0, bias=cfs("halfpi")),
             reads=[r_ang, r_rope, r_cf], writes=[r_rope])
        P.op("act", lambda e: e.activation(out=cosI[:], in_=aq[:, :, 8:12], func=AF.Sin, scale=-1.0, bias=cfs("halfpi")),
             reads=[r_ang, r_rope, r_cf], writes=[r_rope])

        es0.close()
        scope["es"] = es_p12
        P.barrier()
        P.op("pool", lambda e: e.memset(v_sb[:, :, 64:66], 1.0), writes=[r_v])
        P.op("pool", lambda e: e.memset(v_sb[:, :, 66:129], 0.0), reads=[r_v], writes=[r_v])

        es1 = ExitStack()
        scope["es"] = es1
        NXB = 2
        xt = [sb(f"xt{i}", [128, D], F32) for i in range(NXB)]
        r_xt = [Res(f"xt{i}") for i in range(NXB)]
        xsq = sb("xsq", [128, D], BF16)
        r_xsq = Res("xsq")
        ssx = [sb(f"ssx{i}", [128, 1], F32) for i in range(NXB)]
        rsx = [sb(f"rsx{i}", [128, 1], F32) for i in range(NXB)]
        r_ssx = [Res(f"ssx{i}") for i in range(NXB)]
        xn = [sb(f"xn{i}", [128, D], BF16) for i in range(NXB)]
        r_xn = [Res(f"xn{i}") for i in range(NXB)]

        def load_norm_tile(tile, slot, xa, rx):
            P.op("sp", lambda e: e.dma_start(out=xa, in_=x[tile * 128:(tile + 1) * 128, :]),
                 writes=[rx], chan=f"d_{rx.name}")
            P.op("act", lambda e: e.activation(out=xsq[:], in_=xa, func=AF.Square, accum_out=ssx[slot][:]),
                 reads=[rx], writes=[r_xsq, r_ssx[slot]])
            P.op("act", lambda e: e.activation(out=rsx[slot][:], in_=ssx[slot][:], func=AF.Sqrt, scale=1.0 / D,
                                               bias=cfs("eps")), reads=[r_ssx[slot], r_cf], writes=[r_ssx[slot]])
            P.op("dve", lambda e: e.reciprocal(out=ssx[slot][:], in_=rsx[slot][:]), reads=[r_ssx[slot]], writes=[r_ssx[slot]])
            P.op("dve", lambda e: e.scalar_tensor_tensor(out=xn[slot][:], in0=xa, scalar=ssx[slot][:], in1=cfs("gnorm"),
                                                         op0=ALU.mult, op1=ALU.mult),
                 reads=[rx, r_ssx[slot], r_cf], writes=[r_xn[slot]])
            P.guard("dve", [r_xn[slot]])

        hTt = [sb(f"hTt{i}", [128, 8, 128], BF16) for i in range(2)]
        r_hTt = [Res(f"hTt{i}") for i in range(2)]
        sq = sb("sq", [128, 640], F32)
        ssq = sb("ssq", [128, 10], F32)
        rr = sb("rr", [128, 10], F32)
        qn = sb("qn", [128, 10, 64], F32)
        qg = sb("qg", [128, 10, 64], F32)
        qr = sb("qr", [128, 10, 64], BF16)
        qik = sb("qik", [128, 9, 32], F32)
        qikr = sb("qikr", [128, 12, 32], BF16)
        ta = sb("ta", [128, 10, 8], F32)
        tb = sb("tb", [128, 10, 8], F32)
        tc2 = sb("tc2", [128, 10, 8], F32)
        td = sb("td", [128, 10, 8], F32)
        ua = sb("ua", [128, 9, 4], F32)
        ub_ = sb("ub_", [128, 9, 4], F32)
        uc = sb("uc", [128, 9, 4], F32)
        ud = sb("ud", [128, 9, 4], F32)
        witm = sb("witm", [128, 8], F32)
        wexp = sb("wexp", [128, 16], F32)
        r_sq, r_ssq, r_qn, r_qg, r_qr, r_qik, r_qikr, r_t, r_u, r_wi = (
            Res(n) for n in ("sq", "ssq", "qn", "qg", "qr", "qik", "qikr", "t", "u", "wi"))

        B_XT, B_T = 0, 7
        xT_ps = banks[B_XT][:].bitcast(BF16)
        t_ps = banks[B_T][:].bitcast(BF16)
        t1_ps = t_ps[:, 0:640]
        t2_ps = t_ps[:, 640:1024]
        B_T1 = B_T2 = B_T

        def phase1_front(i):
            slot = i % NXB
            hs = i % 2
            B_PA, B_PB, B_PC = 1 + i % 2, 3 + i % 2, 5 + i % 2
            load_norm_tile(i, slot, xt[slot][:], r_xt[slot])
            for kc in range(8):
                P.op("pe", lambda e, kc=kc: e.transpose(out=xT_ps[:, kc * 128:(kc + 1) * 128],
                                                        in_=xn[slot][:, kc * 128:(kc + 1) * 128], identity=ident),
                     reads=[r_xn[slot], r_cb], writes=[bres[B_XT]])
            P.op("dve", lambda e: e.tensor_copy(out=hTt[hs][:].rearrange("p k t -> p (k t)"), in_=xT_ps),
                 reads=[bres[B_XT]], writes=[r_hTt[hs]])
            for kc in range(8):
                st, sp_ = (kc == 0), (kc == 7)
                P.op("pe", lambda e, kc=kc, st=st, sp_=sp_: e.matmul(out=banks[B_PA][:], lhsT=hTt[hs][:, kc, :],
                                                                       rhs=W1[:, kc, 0:512], start=st, stop=sp_),
                     reads=[r_hTt[hs], r_W1c[kc // 2]], writes=[bres[B_PA]])
                P.op("pe", lambda e, kc=kc, st=st, sp_=sp_: e.matmul(out=banks[B_PB][:, 0:424], lhsT=hTt[hs][:, kc, :],
                                                                       rhs=W1[:, kc, 512:936], start=st, stop=sp_),
                     reads=[r_hTt[hs], r_W1c[kc // 2]], writes=[bres[B_PB]])
                P.op("pe", lambda e, kc=kc, st=st, sp_=sp_: e.matmul(out=banks[B_PC][:, 0:128], lhsT=hTt[hs][:, kc, :],
                                                                       rhs=W1[:, kc, 936:1064], start=st, stop=sp_),
                     reads=[r_hTt[hs], r_W1c[kc // 2]], writes=[bres[B_PC]])

        def phase1_back(i):
            B_PA, B_PB, B_PC = 1 + i % 2, 3 + i % 2, 5 + i % 2
            PA, PB, PC = banks[B_PA], banks[B_PB], banks[B_PC]
            P.op("act", lambda e: e.activation(out=v_sb[:, i, 0:64], in_=PC[:, 0:64], func=AF.Copy),
                 reads=[bres[B_PC]], writes=[r_v])
            P.op("act", lambda e: e.activation(out=v_sb[:, i, 129:193], in_=PC[:, 64:128], func=AF.Copy),
                 reads=[bres[B_PC], r_v], writes=[r_v])
            P.op("act", lambda e: e.activation(out=sq[:, 0:512], in_=PA[:], func=AF.Square),
                 reads=[bres[B_PA]], writes=[r_sq])
            P.op("act", lambda e: e.activation(out=sq[:, 512:640], in_=PB[:, 0:128], func=AF.Square),
                 reads=[bres[B_PB], r_sq], writes=[r_sq])
            P.op("dve", lambda e: e.tensor_reduce(out=ssq[:], in_=sq[:].rearrange("p (h d) -> p h d", d=64),
                                                  axis=AX.X, op=ALU.add), reads=[r_sq], writes=[r_ssq])
            P.op("act", lambda e: e.activation(out=rr[:], in_=ssq[:], func=AF.Sqrt, scale=1.0 / 64, bias=cfs("eps")),
                 reads=[r_ssq, r_cf], writes=[r_ssq])
            P.op("dve", lambda e: e.reciprocal(out=ssq[:], in_=rr[:]), reads=[r_ssq], writes=[r_ssq])
            P.op("dve", lambda e: e.tensor_tensor(out=qn[:, 0:8, :], in0=PA[:].rearrange("p (h d) -> p h d", d=64),
                                                  in1=ssq[:, 0:8].unsqueeze(2).to_broadcast([128, 8, 64]), op=ALU.mult),
                 reads=[bres[B_PA], r_ssq], writes=[r_qn])
            P.op("dve", lambda e: e.tensor_tensor(out=qn[:, 8:10, :], in0=PB[:, 0:128].rearrange("p (h d) -> p h d", d=64),
                                                  in1=ssq[:, 8:10].unsqueeze(2).to_broadcast([128, 2, 64]), op=ALU.mult),
                 reads=[bres[B_PB], r_ssq, r_qn], writes=[r_qn])
            P.op("dve", lambda e: e.tensor_tensor(out=qg[:, 0:8, :], in0=qn[:, 0:8, :],
                                                   in1=cfs("gq").unsqueeze(1).to_broadcast([128, 8, 64]), op=ALU.mult),
                 reads=[r_qn, r_cf], writes=[r_qg])
            P.op("dve", lambda e: e.tensor_tensor(out=qg[:, 8:10, :], in0=qn[:, 8:10, :],
                                                   in1=cfs("gk").unsqueeze(1).to_broadcast([128, 2, 64]), op=ALU.mult),
                 reads=[r_qn, r_cf, r_qg], writes=[r_qg])
            cA = cosA[:, i, :].unsqueeze(1).to_broadcast([128, 10, 8])
            sA = sinA[:, i, :].unsqueeze(1).to_broadcast([128, 10, 8])
            t1, t2 = qg[:, :, 0:8], qg[:, :, 8:16]
            P.op("act", lambda e: e.activation(out=qr[:, :, 16:64], in_=qg[:, :, 16:64], func=AF.Copy), reads=[r_qg], writes=[r_qr])
            P.op("dve", lambda e: e.tensor_tensor(out=ta[:], in0=t1, in1=cA, op=ALU.mult), reads=[r_qg, r_rope], writes=[r_t])
            P.op("dve", lambda e: e.tensor_tensor(out=tb[:], in0=t2, in1=sA, op=ALU.mult), reads=[r_qg, r_rope, r_t], writes=[r_t])
            P.op("dve", lambda e: e.tensor_tensor(out=tc2[:], in0=t1, in1=sA, op=ALU.mult), reads=[r_qg, r_rope, r_t], writes=[r_t])
            P.op("dve", lambda e: e.tensor_tensor(out=td[:], in0=t2, in1=cA, op=ALU.mult), reads=[r_qg, r_rope, r_t], writes=[r_t])
            P.op("dve", lambda e: e.tensor_tensor(out=qr[:, :, 0:8], in0=ta[:], in1=tb[:], op=ALU.subtract),
                 reads=[r_t, r_qr], writes=[r_qr])
            P.op("dve", lambda e: e.tensor_tensor(out=qr[:, :, 8:16], in0=tc2[:], in1=td[:], op=ALU.add),
                 reads=[r_t, r_qr], writes=[r_qr])
            P.guard("dve", [r_qr])
            P.op("act", lambda e: e.activation(out=qik[:, 0:8, :].rearrange("p h d -> p (h d)"), in_=PB[:, 128:384],
                                               func=AF.Copy, scale=1.0 / 16.0), reads=[bres[B_PB]], writes=[r_qik])
            P.op("act", lambda e: e.activation(out=qik[:, 8, :], in_=PB[:, 384:416], func=AF.Copy),
                 reads=[bres[B_PB], r_qik], writes=[r_qik])
            P.op("act", lambda e: e.activation(out=witm_all[:, i, :], in_=PB[:, 416:424], func=AF.Copy),
                 reads=[bres[B_PB]], writes=[r_wi])
            cI = cosI[:, i, :].unsqueeze(1).to_broadcast([128, 9, 4])
            sI = sinI[:, i, :].unsqueeze(1).to_broadcast([128, 9, 4])
            u1, u2 = qik[:, :, 0:4], qik[:, :, 4:8]
            P.op("act", lambda e: e.activation(out=qikr[:, 0:9, 8:32], in_=qik[:, :, 8:32], func=AF.Copy), reads=[r_qik], writes=[r_qikr])
            P.op("dve", lambda e: e.tensor_tensor(out=ua[:], in0=u1, in1=cI, op=ALU.mult), reads=[r_qik, r_rope], writes=[r_u])
            P.op("dve", lambda e: e.tensor_tensor(out=ub_[:], in0=u2, in1=sI, op=ALU.mult), reads=[r_qik, r_rope, r_u], writes=[r_u])
            P.op("dve", lambda e: e.tensor_tensor(out=uc[:], in0=u1, in1=sI, op=ALU.mult), reads=[r_qik, r_rope, r_u], writes=[r_u])
            P.op("dve", lambda e: e.tensor_tensor(out=ud[:], in0=u2, in1=cI, op=ALU.mult), reads=[r_qik, r_rope, r_u], writes=[r_u])
            P.op("dve", lambda e: e.tensor_tensor(out=qikr[:, 0:9, 0:4], in0=ua[:], in1=ub_[:], op=ALU.subtract),
                 reads=[r_u, r_qikr], writes=[r_qikr])
            P.op("dve", lambda e: e.tensor_tensor(out=qikr[:, 0:9, 4:8], in0=uc[:], in1=ud[:], op=ALU.add),
                 reads=[r_u, r_qikr], writes=[r_qikr])
            P.op("dve", lambda e: e.tensor_copy(out=qikr[:, 9:12, :], in_=qikr[:, 8:9, :].to_broadcast([128, 3, 32])),
                 reads=[r_qikr], writes=[r_qikr])
            P.guard("dve", [r_qikr])
            qrf = qr[:].rearrange("p h d -> p (h d)")
            for r in range(4):
                P.op("pe", lambda e, r=r: e.transpose(out=t1_ps[:, r * 128:(r + 1) * 128], in_=qrf[:, r * 128:(r + 1) * 128],
                                                      identity=ident), reads=[r_qr, r_cb], writes=[bres[B_T1]])
            P.op("pe", lambda e: e.transpose(out=t1_ps[:, 512:640], in_=qrf[:, 512:640], identity=ident),
                 reads=[r_qr, r_cb], writes=[bres[B_T1]])
            qif = qikr[:].rearrange("p h d -> p (h d)")
            for c in range(3):
                P.op("pe", lambda e, c=c: e.transpose(out=t2_ps[:, c * 128:(c + 1) * 128], in_=qif[:, c * 128:(c + 1) * 128],
                                                      identity=ident), reads=[r_qikr, r_cb], writes=[bres[B_T2]])
            tk = slice(i * 128, (i + 1) * 128)
            P.op("dve", lambda e: e.tensor_copy(out=qT[:, :, tk], in_=t1_ps[:, 0:512].rearrange("p (r t) -> p r t", r=4)),
                 reads=[bres[B_T1]], writes=[r_qT])
            for g_ in range(2):
                P.op("dve", lambda e, g_=g_: e.tensor_scalar(out=kTz[:, g_, tk], in0=t1_ps[:, 512:640], scalar1=cfs("H2", g_, g_ + 1),
                                                             scalar2=None, op0=ALU.mult),
                     reads=[bres[B_T1], r_cf] + ([r_kT] if g_ else []), writes=[r_kT])
            P.op("dve", lambda e: e.tensor_copy(
                out=qiT[:, 2 * i:2 * i + 2, :, :],
                in_=t2_ps[:, 0:256].rearrange("p (h a t) -> p a h t", h=2, a=2)),
                 reads=[bres[B_T2]], writes=[r_qiT])
            P.op("dve", lambda e: e.tensor_copy(out=kiT4[:, tk], in_=t2_ps[:, 256:384]), reads=[bres[B_T2]], writes=[r_kiT])

        n_tiles = NT if "p1_2" not in dbg else 2
        phase1_front(0)
        for i in range(n_tiles):
            if i + 1 < n_tiles:
                phase1_front(i + 1)
            phase1_back(i)
        es1.close()
        P.barrier()
        r_wexp = Res("wexp")
        for hh in range(2):
            P.op("dve", lambda e, hh=hh: e.tensor_tensor(
                out=wexp_all[:].rearrange("p n (a b c) -> p n a b c", a=2, b=2)[:, :, hh, :, :],
                in0=witm_all[:, :, hh * 4:(hh + 1) * 4].unsqueeze(2).to_broadcast([128, NT, 2, 4]),
                in1=cfs("H2").unsqueeze(1).unsqueeze(3).to_broadcast([128, NT, 2, 4]), op=ALU.mult),
                 reads=[r_wi, r_cf, r_wexp], writes=[r_wexp])
        P.op("pe", lambda e: e.matmul(out=banks[0][:], lhsT=cfs("E64"), rhs=wexp_all[:].rearrange("p n c -> p (n c)"),
                                      start=True, stop=True), reads=[r_wexp, r_cf], writes=[bres[0]])
        r_wall = Res("wall")
        for hh in range(2):
            ps = slice(hh * 64, (hh + 1) * 64)
            src = banks[0][ps, :].rearrange("p (n c) -> p n c", c=16)[:, :, hh * 8:hh * 8 + 8]
            P.op("act", lambda e, ps=ps, src=src: e.activation(out=wabs[ps, :, :], in_=src, func=AF.Abs),
                 reads=[bres[0], r_wall], writes=[r_wall])
            P.op("act", lambda e, ps=ps, src=src: e.activation(out=wsgn[ps, :, :], in_=src, func=AF.Sign),
                 reads=[bres[0], r_wall], writes=[r_wall])
        for r_ in r_wabs:
            r_.w = r_wall.w

        es2 = ExitStack()
        scope["es"] = es2
        NSB = 2
        scores = [sb(f"scores{i}", [128, S], F32) for i in range(NSB)]
        r_sc = [Res(f"sc{i}") for i in range(NSB)]
        NMb = [W1_flat[:, i * S:(i + 1) * S] for i in range(NSB)]
        r_NM = [Res(f"NM{i}") for i in range(NSB)]
        Wsel = [sb(f"Wsel{i}", [128, 8, 128], BF16) for i in range(2)]
        r_Wsel = [Res(f"Wsel{i}") for i in range(2)]
        NR = 8
        Rt = [sb(f"Rt{i}", [128, 512], BF16) for i in range(NR)]
        r_R = [Res(f"R{i}") for i in range(NR)]
        NP = 4
        PTt = [sb(f"PTt{i}", [128, 512], BF16) for i in range(NP)]
        r_PT = [Res(f"PT{i}") for i in range(NP)]
        Mx = sb("Mx", [128, 1], F32)
        Mn = sb("Mn", [128, 1], F32)
        Mabs = sb("Mabs", [128, 1], F32)
        bss = sb("bss", [128, NIT], F32)
        bst = sb("bst", [128, NIT], F32)
        mid = [sb(f"mid{i}", [128, 1], F32) for i in range(2)]
        cnt = sb("cnt", [128, 1], F32)
        btmp = sb("btmp", [128, 1], F32)
        bpre = sb("bpre", [128, 1], F32)
        sepscr = sb("sepscr", [128, 32], F32)
        r_bpre = Res("bpre")
        r_bis = Res("bis")
        rdenA = sb("rdenA", [128, 512], F32)
        rdenB = sb("rdenB", [128, 512], F32)
        lnd = sb("lnd", [128, 512], F32)
        r_lnd = Res("lnd")
        bcs = sb("bcs", [128, 512], F32)
        r_rden, r_bcs = Res("rden"), Res("bcs")
        thr_dbg = sb("thr_dbg", [128, NT], F32)

        P.op("dve", lambda e: e.memset(rdenA[:], 0.0), writes=[r_rden])
        P.op("dve", lambda e: e.memset(rdenB[:], 0.0), reads=[r_rden], writes=[r_rden])
        ctr = {"L": 0, "R": 0, "ST": 0, "PT": 0, "relu": 0, "ev": 0, "pool": 0}
        NPB = 5

        def next_pb_idx():
            b = ctr["pool"] % 4
            ctr["pool"] += 1
            return b

        def next_pb():
            b = (5, 7)[ctr["ST"] % 2]
            ctr["ST"] += 1
            return b
        B_IS, B_OT = 4, 6

        def blk_vars(i):
            sbi = i % NSB
            return sbi, i % 2, (i + 1) * 128, i * 128, scores[sbi], NMb[sbi]

        def idx_block(i):
            if i < 2:
                return
            sbi, ws, nk, t0, sc, NM = blk_vars(i)
            nkb = (nk + 511) // 512
            P.op("dve", lambda e: e.tensor_tensor(
                out=Wsel[ws][:].rearrange("p (a b) t -> p a b t", a=2),
                in0=cbs("Dsel2").rearrange("p (a t) -> p a t", a=2).unsqueeze(2).to_broadcast([128, 2, 4, 128]),
                in1=wsgn[:, i, :].rearrange("p (a b) -> p a b", a=2).unsqueeze(3).to_broadcast([128, 2, 4, 128]),
                op=ALU.mult), reads=[r_cb, r_wabs[i]], writes=[r_Wsel[ws]])
            P.guard("dve", [r_Wsel[ws]])

            def kw(kb):
                return min(512, nk - kb * 512)

            grp = [(kb, th) for kb in range(nkb) for th in range(2)]
            lb_of = {}

            def emitL4(n):
                kb, th = grp[n]
                w = kw(kb)
                tq0 = t0 + th * 64
                lbs = [next_pb_idx() for _ in range(4)]
                lb_of[n] = lbs
                for pg in range(4):
                    lb = lbs[pg]
                    P.op("pe", lambda e, pg=pg, lb=lb, w=w, kb=kb, tq0=tq0: e.matmul(
                        out=banks[lb][:, 0:w],
                        lhsT=qiT[pg * 32:(pg + 1) * 32, tq0 // 64, :, :].rearrange("p h t -> p (h t)"),
                        rhs=kiT4[pg * 32:(pg + 1) * 32, kb * 512:kb * 512 + w],
                        start=True, stop=True, tile_position=(pg * 32, 0)),
                         reads=[r_qiT, r_kiT], writes=[bres[lb]])

            emitL4(0)
            for n in range(len(grp)):
                kb, th = grp[n]
                w = kw(kb)
                lbs = lb_of[n]
                rbs = []
                for pg in range(4):
                    j8 = th * 4 + pg
                    lb = lbs[pg]
                    rb = ctr["R"] % NR
                    ctr["R"] += 1
                    rbs.append(rb)
                    wcol = wabs[:, i, j8:j8 + 1]
                    P.op("act", lambda e, lb=lb, rb=rb, wcol=wcol, w=w: e.activation(
                        out=Rt[rb][:, 0:w], in_=banks[lb][:, 0:w], func=AF.Relu, scale=wcol),
                         reads=[bres[lb], r_wabs[i]], writes=[r_R[rb]])
                for pg in range(4):
                    j8 = th * 4 + pg
                    rb = rbs[pg]
                    P.op("pe", lambda e, rb=rb, j8=j8, w=w: e.matmul(out=banks[B_IS][:, 0:w], lhsT=Wsel[ws][:, j8, :],
                                                                rhs=Rt[rb][:, 0:w], start=(j8 == 0), stop=(j8 == 7)),
                         reads=[r_Wsel[ws], r_R[rb]], writes=[bres[B_IS]])
                if n + 1 < len(grp):
                    emitL4(n + 1)
                if th == 1:
                    ks = slice(kb * 512, kb * 512 + w)
                    P.op("act", lambda e, ks=ks, w=w: e.activation(out=sc[:, ks], in_=banks[B_IS][:, 0:w], func=AF.Copy),
                         reads=[bres[B_IS]], writes=[r_sc[sbi]])

        def bis_block(i, fill):
            sbi, ws, nk, t0, sc, NM = blk_vars(i)
            if i < 2:
                if nk > 128:
                    P.op("dve", lambda e: e.memset(NM[:, 0:nk - 128], 0.0), writes=[r_NM[sbi]])
                P.op("dve", lambda e: e.tensor_copy(out=NM[:, nk - 128:nk], in_=cfs("cmaskb")),
                     reads=[r_cf, r_NM[sbi]], writes=[r_NM[sbi]])
                P.guard("dve", [r_NM[sbi]])
                P.play(fill)
                return
            nsl = NIT + 1
            per = (len(fill) + nsl - 1) // nsl
            chunks = [fill[k * per:(k + 1) * per] for k in range(nsl)]
            P.op("dve", lambda e: e.tensor_scalar(out=NM[:, 0:nk], in0=sc[:, 0:nk], scalar1=0.0, scalar2=-3.0e38,
                                                  op0=ALU.add, op1=ALU.max, accum_out=Mx[:]),
                 reads=[r_sc[sbi]], writes=[r_NM[sbi], r_bis])
            P.op("dve", lambda e: e.tensor_scalar(out=NM[:, 0:nk], in0=sc[:, 0:nk], scalar1=0.0, scalar2=3.0e38,
                                                  op0=ALU.add, op1=ALU.min, accum_out=Mn[:]),
                 reads=[r_sc[sbi], r_bis], writes=[r_NM[sbi], r_bis])
            P.op("dve", lambda e: e.memset(mid[1][:], 0.0), reads=[r_bis], writes=[r_bis])
            P.op("dve", lambda e: e.scalar_tensor_tensor(out=Mabs[:], in0=Mn[:], scalar=-1.0, in1=Mx[:],
                                                         op0=ALU.mult, op1=ALU.max), reads=[r_bis], writes=[r_bis])
            P.op("dve", lambda e: e.memset(sepscr[:], 0.0))
            P.op("dve", lambda e: e.tensor_scalar(out=bss[:], in0=cfs("bis_s"), scalar1=Mabs[:, 0:1], scalar2=None,
                                                  op0=ALU.mult), reads=[r_bis, r_cf], writes=[r_bis])
            P.op("dve", lambda e: e.tensor_scalar(out=bst[:], in0=cfs("bis_t"), scalar1=Mabs[:, 0:1], scalar2=None,
                                                  op0=ALU.mult), reads=[r_bis, r_cf], writes=[r_bis])
            P.op("dve", lambda e: e.tensor_tensor(out=sc[:, nk - 128:nk], in0=sc[:, nk - 128:nk], in1=cfs("cmask"),
                                                  op=ALU.add), reads=[r_sc[sbi], r_cf, r_bis], writes=[r_sc[sbi]])
            P.play(chunks[0])

            def sep(k=0):
                for _ in range(k):
                    P.op("dve", lambda e: e.memset(sepscr[:], 0.0))

            for n in range(1, NIT + 1):
                mcur, mnext = mid[n % 2], mid[(n + 1) % 2]
                P.op("dve", lambda e, mcur=mcur: e.tensor_scalar(out=NM[:, 0:nk], in0=sc[:, 0:nk], scalar1=mcur[:, 0:1],
                                                                scalar2=0.0, op0=ALU.is_ge, op1=ALU.add, accum_out=cnt[:]),
                     reads=[r_sc[sbi], r_bis], writes=[r_NM[sbi], r_bis])
                P.op("dve", lambda e, n=n: e.tensor_scalar(out=btmp[:], in0=cnt[:], scalar1=TOPK - 0.5, scalar2=bss[:, n - 1:n],
                                                           op0=ALU.is_ge, op1=ALU.mult), reads=[r_bis], writes=[r_bis])
                P.op("dve", lambda e, n=n, mcur=mcur, mnext=mnext: e.tensor_scalar(
                    out=mnext[:], in0=btmp[:], scalar1=bst[:, n - 1:n], scalar2=mcur[:, 0:1],
                    op0=ALU.add, op1=ALU.add), reads=[r_bis], writes=[r_bis])
                P.play(chunks[n])
            thr = mid[(NIT + 1) % 2]
            if "p2" in dbg:
                P.op("dve", lambda e: e.tensor_copy(out=thr_dbg[:, i:i + 1], in_=thr[:]), reads=[r_bis], writes=[r_bis])
            P.op("dve", lambda e: e.tensor_scalar(out=NM[:, 0:nk], in0=sc[:, 0:nk], scalar1=thr[:, 0:1], scalar2=NEG_MASK,
                                                  op0=ALU.is_lt, op1=ALU.mult), reads=[r_sc[sbi], r_bis], writes=[r_NM[sbi]])
            P.guard("dve", [r_NM[sbi]])

        def att_block(i):
            sbi, ws, nk, t0, sc, NM = blk_vars(i)

            def attn_group(g):
                gp = slice(g * 64, (g + 1) * 64)
                ob = B_OT
                stbank = {}

                def emitQK(j):
                    stb = next_pb()
                    stbank[j] = stb
                    P.op("pe", lambda e: e.matmul(out=banks[stb][:], lhsT=kTz[:, g, j * 128:(j + 1) * 128],
                                                  rhs=qT[:, :, t0:t0 + 128], start=True, stop=False),
                         reads=[r_kT, r_qT], writes=[bres[stb]])
                    P.op("pe", lambda e: e.matmul(out=banks[stb][:], lhsT=NM[:, j * 128:(j + 1) * 128], rhs=cbs("I4"),
                                                  start=False, stop=True),
                         reads=[r_NM[sbi], r_cb], writes=[bres[stb]])

                emitQK(0)
                for j in range(i + 1):
                    stb = stbank[j]
                    pb = ctr["PT"] % NP
                    ctr["PT"] += 1
                    P.op("act", lambda e, stb=stb, pb=pb: e.activation(out=PTt[pb][:], in_=banks[stb][:], func=AF.Exp, scale=0.125),
                         reads=[bres[stb]], writes=[r_PT[pb]])
                    if j + 1 <= i:
                        emitQK(j + 1)
                    lv = v_sb[:, j, 0:128] if g == 0 else v_sb[:, j, 65:193]
                    P.op("pe", lambda e, j=j, pb=pb, lv=lv: e.matmul(out=banks[ob][:, :], lhsT=lv, rhs=PTt[pb][:],
                                                                     start=(j == 0), stop=(j == i)),
                         reads=[r_v, r_PT[pb]], writes=[bres[ob]])
                dp = 64 if g == 0 else 0
                rd = rdenA if g == 0 else rdenB
                P.op("act", lambda e: e.activation(out=lnd[dp:dp + 1, :], in_=banks[ob][dp:dp + 1, :], func=AF.Ln),
                     reads=[bres[ob]], writes=[r_lnd])
                P.op("act", lambda e: e.activation(out=rd[dp:dp + 1, :], in_=lnd[dp:dp + 1, :], func=AF.Exp, scale=-1.0),
                     reads=[r_lnd], writes=[r_rden])
                bcb = next_pb()
                P.op("pe", lambda e: e.matmul(out=banks[bcb][:, :], lhsT=cfs("onesf"), rhs=rd[:, :], start=True, stop=True),
                     reads=[r_rden, r_cf], writes=[bres[bcb]])
                P.op("act", lambda e: e.activation(out=bcs[gp, :], in_=banks[bcb][gp, :], func=AF.Copy),
                     reads=[bres[bcb]], writes=[r_bcs])
                P.op("dve", lambda e: e.tensor_tensor(out=aT[gp, :, t0:t0 + 128],
                                                      in0=banks[ob][gp, :].rearrange("p (r t) -> p r t", r=4),
                                                      in1=bcs[gp, :].rearrange("p (r t) -> p r t", r=4), op=ALU.mult),
                     reads=[bres[ob], r_bcs], writes=[r_aT])

            attn_group(0)
            attn_group(1)

        if "p2f" in dbg:
            p2_blocks = list(range(NT))
        elif "p2" in dbg:
            p2_blocks = [0, 1, 2, 3, 4]
        elif "p1" in dbg or "p1_2" in dbg:
            p2_blocks = []
        else:
            p2_blocks = list(range(NT))
        idx_block(p2_blocks[0]) if p2_blocks else None
        for n_, i in enumerate(p2_blocks):
            P.capture_begin()
            if n_ >= 1:
                att_block(p2_blocks[n_ - 1])
            capA = P.capture_end()
            P.capture_begin()
            if n_ + 1 < len(p2_blocks):
                idx_block(p2_blocks[n_ + 1])
            capI = P.capture_end()
            fill = []
            ia = ii = 0
            na, ni = len(capA), len(capI)
            while ia < na or ii < ni:
                if ii >= ni or (ia < na and ia * max(ni, 1) <= ii * max(na, 1)):
                    fill.append(capA[ia]); ia += 1
                else:
                    fill.append(capI[ii]); ii += 1
            bis_block(i, fill)
        if p2_blocks:
            att_block(p2_blocks[-1])
        es2.close()
        es_p12.close()
        scope["es"] = es
        P.barrier()

        if not ({"p1", "p1_2", "p2", "p2f"} & dbg):
            es3 = ExitStack()
            scope["es"] = es3
            o = 0
            W3 = arena[:, o:o + 8 * 3584].rearrange("p (k c) -> p k c", k=8); o += 8 * 3584
            Wa = arena[:, o:o + 4 * 1024].rearrange("p (k c) -> p k c", k=4); o += 4 * 1024
            Wb = arena[:, o:o + 4 * 1024].rearrange("p (k c) -> p k c", k=4); o += 4 * 1024
            Wo = arena[:, o:o + 8 * 1024].rearrange("p (k c) -> p k c", k=8); o += 8 * 1024
            PW = arena[:, o:o + 4 * 128].rearrange("p (k c) -> p k c", k=4); o += 4 * 128
            r_W3 = [Res(f"W3a_{k}") for k in range(8)]
            r_W3b = [Res(f"W3b_{k}") for k in range(8)]
            r_Wa, r_Wb, r_PW = Res("Wa"), Res("Wb"), Res("PW")
            r_Wo = [Res(f"Wo{k}") for k in range(2)]
            w3v = w3_d.rearrange("(k p) c -> p k c", p=128)
            for kc in range(8):
                P.op("pool", lambda e, kc=kc: e.dma_start(out=W3[:, kc, 0:1536], in_=w3v[:, kc, 0:1536]), writes=[r_W3[kc]], chan=f"d_w3a_{kc}")
            P.op("pool", lambda e: e.dma_start(out=PW, in_=pw_d.rearrange("(k p) c -> p k c", p=128)), writes=[r_PW], chan="d_pw")
            P.op("pool", lambda e: e.dma_start(out=Wa, in_=wa_d.rearrange("(k p) c -> p k c", p=128)), writes=[r_Wa], chan="d_wa")
            P.op("pool", lambda e: e.dma_start(out=Wb, in_=wb_d.rearrange("(k p) c -> p k c", p=128)), writes=[r_Wb], chan="d_wb")
            for kc in range(8):
                P.op("pool", lambda e, kc=kc: e.dma_start(out=W3[:, kc, 1536:3584], in_=w3v[:, kc, 1536:3584]), writes=[r_W3b[kc]], chan=f"d_w3b_{kc}")
            wov = wo_d.rearrange("(k p) c -> p k c", p=128)
            for hk in range(2):
                P.op("pool", lambda e, hk=hk: e.dma_start(out=Wo[:, 4 * hk:4 * hk + 4, :], in_=wov[:, 4 * hk:4 * hk + 4, :]),
                     writes=[r_Wo[hk]], chan=f"d_wo{hk}")

            xblk = sb("xblk", [128, 4, D], F32)
            r_xb = [Res(f"xb{q}") for q in range(4)]
            xsq3 = sb("xsq3", [128, D], BF16)
            ssx3 = sb("ssx3", [128, 1], F32)
            rsx3 = sb("rsx3", [128, 1], F32)
            xn3s = [sb(f"xn3_{k}", [128, D], BF16) for k in range(2)]
            r_xn3s = [Res(f"xn3_{k}") for k in range(2)]
            r_ssx3, r_xsq3 = Res("ssx3"), Res("xsq3")
            hTb = sb("hTb", [128, 8, 512], BF16)
            r_hTb = Res("hTb")
            zs2 = [arena[:, o + k * 512:o + (k + 1) * 512] for k in range(2)]
            r_zs2 = [Res(f"zs{k}") for k in range(2)]
            agT = sb("agT", [128, 4, 512], BF16)
            r_ag = Res("ag")
            uT = sb("uT", [128, 4, 528], F32)
            r_u3 = [Res(f"u3_{g}") for g in range(4)]
            pa = sb("pa", [128, 528], F32)
            pb2 = sb("pb2", [128, 528], F32)
            r_pp = Res("pp")
            tmp16 = sb("tmp16", [128, 16], F32)
            pooledT = sb("pooledT", [128, 4, 512], BF16)
            r_pool = [Res(f"pool{g}") for g in range(4)]
            zbs = sb("zbs", [128, 512], BF16)
            r_zbs = Res("zbs")
            bT = sb("bT", [128, 4, 512], BF16)
            r_bT = Res("bT")
            sgA2 = [arena[:, o + 1024:o + 1536], sb("sgA1", [128, 512], BF16)]
            sgB_ = sb("sgB0", [128, 512], BF16)
            sgB2 = [sgB_, sgB_]
            r_sgA2 = [Res(f"sgA{k}") for k in range(2)]
            r_sgB_ = Res("sgB")
            r_sgB2 = [r_sgB_, r_sgB_]
            tA = sb("tA", [128, 512], F32)
            tB = sb("tB", [128, 512], F32)
            r_tA, r_tB = Res("tA"), Res("tB")
            yT = arena[:, o + 1536:o + 1536 + 8 * 512].rearrange("p (k t) -> p k t", k=8)
            r_yT = [Res(f"yT{k}") for k in range(8)]
            ot = sb("ot", [128, D], F32)
            r_ot = Res("ot")
            pj = {"n": 0, "o": 0}
            xT3_ps = banks[0][:].bitcast(BF16)
            out_toks = []

            def next_pj():
                b = 1 + pj["n"] % 7
                pj["n"] += 1
                return b

            def proj_fm(col0, reads_extra=()):
                b = next_pj()
                for kc in range(8):
                    P.op("pe", lambda e, kc=kc, b=b: e.matmul(out=banks[b][:], lhsT=W3[:, kc, col0:col0 + 128], rhs=hTb[:, kc, :],
                                                              start=(kc == 0), stop=(kc == 7)),
                         reads=[(r_W3[kc] if col0 < 1536 else r_W3b[kc]), r_hTb], writes=[bres[b]])
                return b

            def phase3_block(tb):
                tok0 = tb * 512
                tsl = slice(tok0, tok0 + 512)
                for q in range(4):
                    tile = 4 * tb + q
                    xa = xblk[:, q, :]
                    xn3, r_xn3 = xn3s[q % 2], r_xn3s[q % 2]
                    P.op("sp", lambda e, xa=xa, tile=tile: e.dma_start(out=xa, in_=x[tile * 128:(tile + 1) * 128, :]),
                         writes=[r_xb[q]], chan=f"d_xb{q}")
                    P.op("act", lambda e, xa=xa: e.activation(out=xsq3[:], in_=xa, func=AF.Square, accum_out=ssx3[:]),
                         reads=[r_xb[q]], writes=[r_xsq3, r_ssx3])
                    P.op("act", lambda e: e.activation(out=rsx3[:], in_=ssx3[:], func=AF.Sqrt, scale=1.0 / D, bias=cfs("eps")),
                         reads=[r_ssx3, r_cf], writes=[r_ssx3])
                    P.op("dve", lambda e: e.reciprocal(out=ssx3[:], in_=rsx3[:]), reads=[r_ssx3], writes=[r_ssx3])
                    P.op("dve", lambda e, xa=xa, xn3=xn3: e.scalar_tensor_tensor(out=xn3[:], in0=xa, scalar=ssx3[:], in1=cfs("gnorm"),
                                                                        op0=ALU.mult, op1=ALU.mult),
                         reads=[r_xb[q], r_ssx3, r_cf], writes=[r_xn3])
                    P.guard("dve", [r_xn3])
                    for kc in range(8):
                        P.op("pe", lambda e, kc=kc, xn3=xn3: e.transpose(out=xT3_ps[:, kc * 128:(kc + 1) * 128],
                                                                in_=xn3[:, kc * 128:(kc + 1) * 128], identity=ident),
                             reads=[r_xn3, r_cb], writes=[bres[0]])
                    P.op("dve", lambda e, q=q: e.tensor_copy(out=hTb[:, :, q * 128:(q + 1) * 128],
                                                             in_=xT3_ps.rearrange("p (k t) -> p k t", k=8)),
                         reads=[bres[0]], writes=[r_hTb])
                for c in range(4):
                    b = proj_fm(c * 128)
                    zs, r_zs = zs2[c % 2], r_zs2[c % 2]
                    P.op("act", lambda e, b=b, zs=zs: e.activation(out=zs[:], in_=banks[b][:], func=AF.Silu),
                         reads=[bres[b]], writes=[r_zs])
                    P.op("dve", lambda e, c=c, zs=zs: e.tensor_tensor(out=agT[:, c, :], in0=zs[:], in1=aT[:, c, tsl], op=ALU.mult),
                         reads=[r_zs, r_aT], writes=[r_ag])
                if tb == 0:
                    P.op("dve", lambda e: e.memset(uT[:, :, 0:16], 0.0), writes=r_u3)
                ub_bank = {0: proj_fm(512)}
                for g in range(4):
                    b = ub_bank[g]
                    P.op("act", lambda e, b=b, g=g: e.activation(out=uT[:, g, 16:528], in_=banks[b][:], func=AF.Copy),
                         reads=[bres[b]], writes=[r_u3[g]])
                    U = uT[:, g, :]
                    wdw = 2 ** (g + 1)
                    P.op("dve", lambda e, U=U: e.tensor_tensor(out=pa[:, 1:528], in0=U[:, 1:528], in1=U[:, 0:527], op=ALU.add),
                         reads=[r_u3[g]], writes=[r_pp])
                    last = pa
                    if g >= 1:
                        P.op("dve", lambda e: e.tensor_tensor(out=pb2[:, 3:528], in0=pa[:, 3:528], in1=pa[:, 1:526], op=ALU.add),
                             reads=[r_pp], writes=[r_pp])
                        last = pb2
                    if g >= 2:
                        P.op("dve", lambda e: e.tensor_tensor(out=pa[:, 7:528], in0=pb2[:, 7:528], in1=pb2[:, 3:524], op=ALU.add),
                             reads=[r_pp], writes=[r_pp])
                        last = pa
                    if g >= 3:
                        P.op("dve", lambda e: e.tensor_tensor(out=pb2[:, 15:528], in0=pa[:, 15:528], in1=pa[:, 7:520], op=ALU.add),
                             reads=[r_pp], writes=[r_pp])
                        last = pb2
                    P.op("dve", lambda e, g=g, U=U, last=last, wdw=wdw: e.scalar_tensor_tensor(
                        out=pooledT[:, g, :], in0=last[:, 16:528], scalar=1.0 / wdw, in1=U[:, 16:528],
                        op0=ALU.mult, op1=ALU.subtract), reads=[r_pp, r_u3[g]], writes=[r_pool[g]])
                    if tb == 0:
                        P.op("dve", lambda e, g=g, last=last: e.tensor_tensor(out=tmp16[:], in0=last[:, 16:32],
                                                                              in1=cfs("invcnt", g * 16, g * 16 + 16), op=ALU.mult),
                             reads=[r_pp, r_cf], writes=[r_pp])
                        P.op("dve", lambda e, g=g, U=U: e.tensor_tensor(out=pooledT[:, g, 0:16], in0=tmp16[:], in1=U[:, 16:32],
                                                                        op=ALU.subtract),
                             reads=[r_pp, r_u3[g], r_pool[g]], writes=[r_pool[g]])
                    P.op("dve", lambda e, g=g: e.tensor_copy(out=uT[:, g, 0:16], in_=uT[:, g, 512:528]),
                         reads=[r_u3[g], r_pp], writes=[r_u3[g]])
                    if g + 1 < 4:
                        ub_bank[g + 1] = proj_fm(512 + (g + 1) * 128)
                    bz = proj_fm(1024 + g * 128)
                    P.op("act", lambda e, bz=bz: e.activation(out=zbs[:], in_=banks[bz][:], func=AF.Silu),
                         reads=[bres[bz]], writes=[r_zbs])
                    bm = next_pj()
                    P.op("pe", lambda e, g=g, bm=bm: e.matmul(out=banks[bm][:], lhsT=PW[:, g, :], rhs=pooledT[:, g, :],
                                                              start=True, stop=True),
                         reads=[r_PW, r_pool[g]], writes=[bres[bm]])
                    P.op("dve", lambda e, g=g, bm=bm: e.scalar_tensor_tensor(
                        out=bT[:, g, :], in0=banks[bm][:], scalar=cfs("pscale", g, g + 1), in1=zbs[:],
                        op0=ALU.mult, op1=ALU.mult), reads=[bres[bm], r_zbs, r_cf], writes=[r_bT])
                for m in range(8):
                    msl = slice(m * 128, (m + 1) * 128)
                    sgA, sgB, r_sgA, r_sgB = sgA2[m % 2], sgB2[m % 2], r_sgA2[m % 2], r_sgB2[m % 2]
                    bA = next_pj()
                    for c in range(4):
                        P.op("pe", lambda e, c=c, bA=bA, msl=msl: e.matmul(out=banks[bA][:], lhsT=Wa[:, c, msl], rhs=agT[:, c, :],
                                                                           start=(c == 0), stop=(c == 3)),
                             reads=[r_Wa, r_ag], writes=[bres[bA]])
                    bgA = proj_fm(1536 + m * 128)
                    P.op("act", lambda e, bgA=bgA, m=m, sgA=sgA: e.activation(out=sgA[:], in_=banks[bgA][:], func=AF.Sigmoid,
                                                                     bias=cfs("mbias", m, m + 1)),
                         reads=[bres[bgA], r_cf], writes=[r_sgA])
                    P.op("dve", lambda e, bA=bA, sgA=sgA: e.tensor_tensor(out=tA[:], in0=banks[bA][:], in1=sgA[:], op=ALU.mult),
                         reads=[bres[bA], r_sgA], writes=[r_tA])
                    bB = next_pj()
                    for c in range(4):
                        P.op("pe", lambda e, c=c, bB=bB, msl=msl: e.matmul(out=banks[bB][:], lhsT=Wb[:, c, msl], rhs=bT[:, c, :],
                                                                           start=(c == 0), stop=(c == 3)),
                             reads=[r_Wb, r_bT], writes=[bres[bB]])
                    bgB = proj_fm(1536 + 1024 + m * 128)
                    P.op("act", lambda e, bgB=bgB, m=m, sgB=sgB: e.activation(out=sgB[:], in_=banks[bgB][:], func=AF.Sigmoid,
                                                                     bias=cfs("mbias", 8 + m, 9 + m)),
                         reads=[bres[bgB], r_cf], writes=[r_sgB])
                    P.op("dve", lambda e, bB=bB, sgB=sgB: e.tensor_tensor(out=tB[:], in0=banks[bB][:], in1=sgB[:], op=ALU.mult),
                         reads=[bres[bB], r_sgB], writes=[r_tB])
                    P.op("dve", lambda e, m=m: e.tensor_tensor(out=yT[:, m, :], in0=tA[:], in1=tB[:], op=ALU.add),
                         reads=[r_tA, r_tB], writes=[r_yT[m]])
                    if m == 7:
                        P.guard("dve", [r_yT[7]])
                for q in range(4):
                    for half in range(2):
                        bo = 5 + pj["o"] % 3
                        pj["o"] += 1
                        hs_ = slice(half * 512, (half + 1) * 512)
                        for kc in range(8):
                            P.op("pe", lambda e, kc=kc, bo=bo, q=q, hs_=hs_: e.matmul(
                                out=banks[bo][:], lhsT=yT[:, kc, q * 128:(q + 1) * 128], rhs=Wo[:, kc, hs_],
                                start=(kc == 0), stop=(kc == 7)),
                                 reads=[r_yT[kc], r_Wo[kc // 4]], writes=[bres[bo]])
                        P.op("dve", lambda e, bo=bo, q=q, hs_=hs_: e.tensor_tensor(out=ot[:, hs_], in0=banks[bo][:], in1=xblk[:, q, hs_],
                                                                                   op=ALU.add),
                             reads=[bres[bo], r_xb[q]], writes=[r_ot])
                    tile = 4 * tb + q
                    out_toks.append(P.op("sp", lambda e, tile=tile: e.dma_start(out=out_d[tile * 128:(tile + 1) * 128, :], in_=ot[:]),
                                         reads=[r_ot], chan="d_out"))

            for tbi in range(8):
                phase3_block(tbi)
            es3.close()
            scope["es"] = es

        finals = []
        if not ({"p1", "p1_2", "p2", "p2f"} & dbg):
            finals.append(out_toks[-1])
        if "p1" in dbg or "p1_2" in dbg:
            def dump(name, ap, res, dt=F32, shape=None):
                d = dout(name, shape or list(ap.shape), dt)
                finals.append(P.op("sp", lambda e: e.dma_start(out=d, in_=ap), reads=res, chan="d_dbg_" + name))
            dump("d_cosA", cosA[:], [r_rope])
            dump("d_sinA", sinA[:], [r_rope])
            dump("d_cosI", cosI[:], [r_rope])
            dump("d_sinI", sinI[:], [r_rope])
            dump("d_qT", qT, [r_qT], BF16, [128, 4, S])
            dump("d_kT", kTz, [r_kT], BF16, [128, 2, S])
            dump("d_qiT", qiT, [r_qiT], BF16, [128, 64, 2, 64])
            dump("d_kiT4", kiT4, [r_kiT], BF16, [128, S])
            dump("d_v", v_sb, [r_v], BF16, [128, NT, 193])
            dump("d_wabs", wabs[:], r_wabs)
            dump("d_wsgn", wsgn[:], r_wabs)
        if "p2" in dbg:
            def dump2(name, ap, res, dt=F32, shape=None):
                d = dout(name, shape or list(ap.shape), dt)
                finals.append(P.op("sp", lambda e: e.dma_start(out=d, in_=ap), reads=res, chan="d_dbg_" + name))
            dump2("d_aT", aT[:], [r_aT], BF16)
            dump2("d_thr", thr_dbg[:], [r_bis])
            dump2("d_sc", scores[4 % NSB][:], [r_sc[4 % NSB]])
            dump2("d_NM", NMb[4 % NSB][:], [r_NM[4 % NSB]], BF16)
        if "p2f" in dbg:
            d_ = dout("d_thr", [128, NT], F32)
            finals.append(P.op("sp", lambda e: e.dma_start(out=d_, in_=thr_dbg[:]), reads=[r_bis], chan="d_dbg_thr"))
            d2_ = dout("d_aT", [128, 4, S], BF16)
            finals.append(P.op("sp", lambda e: e.dma_start(out=d2_, in_=aT[:]), reads=[r_aT], chan="d_dbg_aT"))
            d5_ = dout("d_qiT", [128, 2 * S], BF16)
            finals.append(P.op("sp", lambda e: e.dma_start(out=d5_, in_=qiT.rearrange("p b h t -> p (b h t)")), reads=[r_qiT], chan="d_dbg_qiT"))
            d6_ = dout("d_wabs", [128, NT, 8], F32)
            finals.append(P.op("sp", lambda e: e.dma_start(out=d6_, in_=wabs[:]), reads=r_wabs, chan="d_dbg_wabs"))
            d7_ = dout("d_wsgn", [128, NT, 8], F32)
            finals.append(P.op("sp", lambda e: e.dma_start(out=d7_, in_=wsgn[:]), reads=r_wabs, chan="d_dbg_wsgn"))
            d3_ = dout("d_sc", [128, S], F32)
            finals.append(P.op("sp", lambda e: e.dma_start(out=d3_, in_=scores[1][:]), reads=[r_sc[1]], chan="d_dbg_sc"))
            d4_ = dout("d_NM", [128, S], BF16)
            finals.append(P.op("sp", lambda e: e.dma_start(out=d4_, in_=NMb[1]), reads=[r_NM[1]], chan="d_dbg_NM"))
        P.final_wait("sp", finals)
        P.emit()
    return nc, list(dbg_d.keys())


def host_inputs(inputs):
    f = np.float32
    w_in = np.asarray(inputs["w_in"], f)[0]
    pq = _perm_q()
    sp = np.cumsum([512, 128, 128, 256, 32, 8, 512, 512, 512, 2048])
    q, k, v, qi, ki, wi, za, ub, zb, gates = np.split(w_in, sp[:-1], axis=1)
    w1 = np.ascontiguousarray(np.concatenate([q[:, pq], k, qi, ki, wi, v], axis=1))
    w3 = np.ascontiguousarray(np.concatenate([za[:, pq], ub, zb, gates], axis=1))
    wa = np.ascontiguousarray(np.asarray(inputs["w_branch_a"], f)[0][pq, :])
    wb = np.ascontiguousarray(np.asarray(inputs["w_branch_b"], f)[0])
    wo = np.ascontiguousarray(np.asarray(inputs["w_out"], f)[0])
    pw = np.ascontiguousarray(np.asarray(inputs["pool_w"], f)[0].reshape(512, 128))
    cf, cb = host_consts()
    o, w = CF["gnorm"]; cf[:, o:o + w] = np.asarray(inputs["norm_g"], f)[0][None, :]
    o, w = CF["gq"]; cf[:, o:o + w] = np.asarray(inputs["q_norm_g"], f)[0][None, :]
    o, w = CF["gk"]; cf[:, o:o + w] = np.asarray(inputs["k_norm_g"], f)[0][None, :]
    mb = np.asarray(inputs["merge_bias"], f)[0]
    o, w = CF["mbias"]; cf[:, o:o + w] = mb.reshape(2, 8, 128).transpose(2, 0, 1).reshape(128, 16)
    ps = np.asarray(inputs["pool_scale"], f)[0]
    o, w = CF["pscale"]; cf[:, o:o + w] = ps.reshape(4, 128).T
    xs = np.asarray(inputs["x"], f)
    pos = np.asarray(inputs["positions"], np.int32)
    maps = []
    for b in range(8):
        maps.append({
            "x": np.ascontiguousarray(xs[b]),
            "posT": np.ascontiguousarray(pos[b].reshape(NT, 128).T),
            "cf": cf, "cb": cb, "w1": w1, "w3": w3, "wa": wa, "wb": wb, "wo": wo, "pw": pw,
        })
    return maps


def kernel(**inputs):
    maps = host_inputs(inputs)
    nc, _ = build()
    res = run_bass_kernel_spmd(nc, maps, core_ids=list(range(8)))
    return np.stack([np.asarray(r["out"], np.float32) for r in res.results], axis=0)
```
